# Optimizing a Trainium2 kernel written in Bass

```python
import math
import numpy as np
import jax
import jax.numpy as jnp
from jax import lax

D_MODEL = 1024
BATCH = 8
SEQ = 2048
DEPTH = 4

CTX_LEN = 256
GRID_W = 64
EPS = 1e-6
N_MOD = 6

SSD_D_INNER = 2 * D_MODEL
SSD_HEAD_DIM = 64
SSD_HEADS = SSD_D_INNER // SSD_HEAD_DIM
SSD_GROUPS = 8
SSD_HPG = SSD_HEADS // SSD_GROUPS
SSD_STATE = 128
SSD_CONV = 5
SSD_CHUNK = 128
SSD_XBC = SSD_D_INNER + 2 * SSD_GROUPS * SSD_STATE

NA_HEAD_DIM = 64
NA_HEADS = D_MODEL // NA_HEAD_DIM
NA_WIDTH = NA_HEADS * NA_HEAD_DIM
NA_MAX_KH = 8
NA_KW = 16
NA_QCB = NA_KW
NA_KCB = 2 * NA_KW
NA_NCB = GRID_W // NA_QCB

SWA_HEAD_DIM = 64
SWA_HEADS = D_MODEL // SWA_HEAD_DIM
SWA_KV_HEADS = SWA_HEADS // 4
SWA_GQA = SWA_HEADS // SWA_KV_HEADS
SWA_WINDOW = 128
SWA_Q_WIDTH = SWA_HEADS * SWA_HEAD_DIM
SWA_KV_WIDTH = SWA_KV_HEADS * SWA_HEAD_DIM
ROPE_BASE = 10000.0

D_FF = 4 * D_MODEL
N_BRANCHES = 3

XBC_OFF = 0
DT_OFF = XBC_OFF + SSD_XBC
NA_K_OFF = DT_OFF + 2 * SSD_HEADS
NA_V_OFF = NA_K_OFF + NA_WIDTH
SWA_K_OFF = NA_V_OFF + NA_WIDTH
SWA_V_OFF = SWA_K_OFF + SWA_KV_WIDTH
CTX_COLS = SWA_V_OFF + SWA_KV_WIDTH
Z_OFF = CTX_COLS
NA_Q_OFF = Z_OFF + SSD_D_INNER
SWA_Q_OFF = NA_Q_OFF + NA_WIDTH
GATE_OFF = SWA_Q_OFF + SWA_Q_WIDTH
IN_COLS = GATE_OFF + N_BRANCHES * D_MODEL

kernel_name = "hybrid_ssd_na_swa_dit_trunk"


def rmsnorm(x, g):
    xf = x.astype(jnp.float32)
    y = xf * lax.rsqrt(jnp.mean(xf * xf, axis=-1, keepdims=True) + EPS)
    return y.astype(x.dtype) * g


def group_rmsnorm(x, g, groups):
    shp = x.shape
    xf = x.astype(jnp.float32).reshape(shp[:-1] + (groups, shp[-1] // groups))
    y = xf * lax.rsqrt(jnp.mean(xf * xf, axis=-1, keepdims=True) + EPS)
    return y.reshape(shp).astype(x.dtype) * g


def rope_1d(t, pos):
    half = t.shape[-1] // 2
    inv = ROPE_BASE ** (-jnp.arange(half, dtype=jnp.float32) / half)
    ang = pos.astype(jnp.float32)[:, None] * inv[None, :]
    cos = jnp.cos(ang)[:, None, :]
    sin = jnp.sin(ang)[:, None, :]
    t1 = t[..., :half].astype(jnp.float32)
    t2 = t[..., half:].astype(jnp.float32)
    return jnp.concatenate([t1 * cos - t2 * sin, t1 * sin + t2 * cos], axis=-1).astype(t.dtype)


def axial_rope(t):
    L = t.shape[1]
    pos = jnp.arange(L)
    half = t.shape[-1] // 2
    return jnp.concatenate([rope_1d(t[..., :half], pos // GRID_W),
                            rope_1d(t[..., half:], pos % GRID_W)], axis=-1)


def depthwise_conv_centred(u, w, b):
    pad = w.shape[0] // 2
    y = lax.conv_general_dilated(u, w[:, None, :].astype(u.dtype), window_strides=(1,),
                                 padding=[(pad, pad)], dimension_numbers=('NWC', 'WIO', 'NWC'),
                                 feature_group_count=u.shape[-1])
    return y + b


def heads(p, off, n, d):
    return p[..., off:off + n * d].reshape(p.shape[:2] + (n, d))


def ssd_chunked(x, dt, A, Bm, Cm, init, return_y):
    b, L = x.shape[:2]
    Q = SSD_CHUNK
    nc = L // Q
    G, R, P, N = SSD_GROUPS, SSD_HPG, SSD_HEAD_DIM, SSD_STATE
    xc = x.reshape(b, nc, Q, G, R, P)
    dtc = dt.reshape(b, nc, Q, G, R)
    Bc = Bm.reshape(b, nc, Q, G, N)
    Cc = Cm.reshape(b, nc, Q, G, N)
    acum = jnp.cumsum(dtc * A, axis=2)
    decay_to_end = jnp.exp(acum[:, :, -1:] - acum)
    chunk_states = jnp.einsum('bcqgn,bcqgrp->bcgrpn', Bc, xc * (decay_to_end * dtc)[..., None])
    chunk_decay = jnp.exp(acum[:, :, -1])

    def step(s, inp):
        st, dec = inp
        return s * dec[..., None, None] + st, s

    final, s_in = lax.scan(step, init, (chunk_states.swapaxes(0, 1), chunk_decay.swapaxes(0, 1)))
    if not return_y:
        return None, final
    s_in = s_in.swapaxes(0, 1)
    idx = jnp.arange(Q)
    lower = idx[:, None] >= idx[None, :]
    seg = acum[:, :, :, None] - acum[:, :, None, :]
    lmat = jnp.exp(jnp.where(lower[:, :, None, None], seg, -jnp.inf))
    cb = jnp.einsum('bcign,bcjgn->bcijg', Cc, Bc)
    m = cb[..., None] * lmat * dtc[:, :, None]
    y_diag = jnp.einsum('bcijgr,bcjgrp->bcigrp', m, xc)
    y_off = jnp.einsum('bcign,bcgrpn->bcigrp', Cc, s_in) * jnp.exp(acum)[..., None]
    return (y_diag + y_off).reshape(b, L, G, R, P), final


def ssd_inputs(p, conv_w, conv_b):
    b, L = p.shape[:2]
    G, R, P, N = SSD_GROUPS, SSD_HPG, SSD_HEAD_DIM, SSD_STATE
    xbc = jax.nn.silu(depthwise_conv_centred(p[..., XBC_OFF:XBC_OFF + SSD_XBC], conv_w, conv_b))
    xs = xbc[..., :SSD_D_INNER].reshape(b, L, G, R, P)
    Bm = xbc[..., SSD_D_INNER:SSD_D_INNER + G * N].reshape(b, L, G, N)
    Cm = xbc[..., SSD_D_INNER + G * N:].reshape(b, L, G, N)
    dt_raw = p[..., DT_OFF:DT_OFF + 2 * SSD_HEADS].reshape(b, L, 2, G, R)
    return xs, Bm, Cm, dt_raw


def ssd_bidirectional(xs, Bm, Cm, dt_raw, dt_bias, A, init_f, init_b, return_y):
    f32 = jnp.float32
    xf, Bf, Cf = xs.astype(f32), Bm.astype(f32), Cm.astype(f32)
    dtb = dt_bias.astype(f32).reshape(2, SSD_GROUPS, SSD_HPG)
    dt_f = jax.nn.softplus(dt_raw[:, :, 0].astype(f32) + dtb[0])
    dt_b = jax.nn.softplus(dt_raw[:, :, 1].astype(f32) + dtb[1])
    rev = lambda t: jnp.flip(t, axis=1)
    y_f, s_f = ssd_chunked(xf, dt_f, A[0], Bf, Cf, init_f, return_y)
    y_b, s_b = ssd_chunked(rev(xf), rev(dt_b), A[1], rev(Bf), rev(Cf), init_b, return_y)
    y = (y_f + rev(y_b)) if return_y else None
    return y, s_f, s_b


def ssd_output(y, xs, d_skip, z, norm_g):
    b, L = y.shape[:2]
    y = y + d_skip.astype(jnp.float32).reshape(SSD_GROUPS, SSD_HPG)[..., None] * xs.astype(jnp.float32)
    y = y.reshape(b, L, SSD_D_INNER).astype(z.dtype)
    return group_rmsnorm(y * jax.nn.silu(z), norm_g, SSD_GROUPS)


def na_latent(q, k, v, kc, vc, rpb):
    b, L, H, d = q.shape
    rows = L // GRID_W
    kh = min(NA_MAX_KH, rows)
    scale = d ** -0.5
    kg = k.reshape(b, rows, GRID_W, H, d)
    vg = v.reshape(b, rows, GRID_W, H, d)
    qr = q.reshape(b, rows, NA_NCB, NA_QCB, H, d).swapaxes(0, 1)
    kstart = [int(np.clip(n * NA_QCB - NA_KW // 2, 0, GRID_W - NA_KCB)) for n in range(NA_NCB)]
    qcol = np.arange(NA_NCB)[:, None, None] * NA_QCB + np.arange(NA_QCB)[None, :, None]
    kcol = np.array(kstart)[:, None, None] + np.arange(NA_KCB)[None, None, :]
    cs = np.clip(qcol - NA_KW // 2, 0, GRID_W - NA_KW)
    col_valid = (kcol >= cs) & (kcol < cs + NA_KW)
    col_idx = np.clip(kcol - qcol + NA_KW - 1, 0, 2 * NA_KW - 2)
    n_loc = kh * NA_KCB

    def row_block(args):
        r, q_r = args
        rs = jnp.clip(r - kh // 2, 0, rows - kh)
        k_r = lax.dynamic_slice_in_dim(kg, rs, kh, axis=1)
        v_r = lax.dynamic_slice_in_dim(vg, rs, kh, axis=1)
        k_b = jnp.stack([k_r[:, :, s:s + NA_KCB] for s in kstart], axis=1)
        v_b = jnp.stack([v_r[:, :, s:s + NA_KCB] for s in kstart], axis=1)
        row_idx = rs + jnp.arange(kh) - r + NA_MAX_KH - 1
        bias = rpb[:, row_idx[None, None, :, None], col_idx[:, :, None, :]]
        s_loc = jnp.einsum('bnqhd,bnijhd->bhnqij', q_r, k_b) * scale + bias
        s_loc = jnp.where(col_valid[:, :, None, :], s_loc, -jnp.inf).reshape(b, H, NA_NCB, NA_QCB, n_loc)
        s_ctx = jnp.einsum('bnqhd,bchd->bhnqc', q_r, kc) * scale
        p = jax.nn.softmax(jnp.concatenate([s_loc, s_ctx], axis=-1).astype(jnp.float32), axis=-1).astype(v.dtype)
        o = (jnp.einsum('bhnqk,bnkhd->bnqhd', p[..., :n_loc], v_b.reshape(b, NA_NCB, n_loc, H, d))
             + jnp.einsum('bhnqc,bchd->bnqhd', p[..., n_loc:], vc))
        return o.reshape(b, GRID_W, H, d)

    out = lax.map(row_block, (jnp.arange(rows), qr))
    return out.swapaxes(0, 1).reshape(b, L, H, d)


def ctx_mha(q, k, v):
    s = jnp.einsum('bqhd,bkhd->bhqk', q, k) * (q.shape[-1] ** -0.5)
    p = jax.nn.softmax(s.astype(jnp.float32), axis=-1).astype(v.dtype)
    return jnp.einsum('bhqk,bkhd->bqhd', p, v)


def swa_latent(q, k, v, kc, vc, sink):
    b, L, G, R, d = q.shape
    W = SWA_WINDOW
    nb = L // W
    scale = d ** -0.5
    kp = jnp.pad(k, ((0, 0), (W, W), (0, 0), (0, 0)))
    vp = jnp.pad(v, ((0, 0), (W, W), (0, 0), (0, 0)))
    qb = q.reshape(b, nb, W, G, R, d).swapaxes(0, 1)
    rel = np.arange(3 * W)[None, :] - W - np.arange(W)[:, None]
    band = np.abs(rel) <= W
    sink_b = jnp.broadcast_to(sink.reshape(1, G, R, 1, 1), (b, G, R, W, 1))

    def block(args):
        i, q_i = args
        k_i = lax.dynamic_slice_in_dim(kp, i * W, 3 * W, axis=1)
        v_i = lax.dynamic_slice_in_dim(vp, i * W, 3 * W, axis=1)
        kpos = (i - 1) * W + jnp.arange(3 * W)
        valid = band & ((kpos >= 0) & (kpos < L))[None, :]
        s_loc = jnp.where(valid, jnp.einsum('bqgrd,bkgd->bgrqk', q_i, k_i) * scale, -jnp.inf)
        s_ctx = jnp.einsum('bqgrd,bkgd->bgrqk', q_i, kc) * scale
        logits = jnp.concatenate([s_loc, s_ctx, sink_b.astype(s_loc.dtype)], axis=-1).astype(jnp.float32)
        p = jax.nn.softmax(logits, axis=-1).astype(v.dtype)
        return (jnp.einsum('bgrqk,bkgd->bqgrd', p[..., :3 * W], v_i)
                + jnp.einsum('bgrqk,bkgd->bqgrd', p[..., 3 * W:-1], vc))

    out = lax.map(block, (jnp.arange(nb), qb))
    return out.swapaxes(0, 1).reshape(b, L, G, R, d)


def ctx_swa_sink(q, k, v, sink):
    b, C, G, R, d = q.shape
    s = jnp.einsum('bqgrd,bkgd->bgrqk', q, k) * (d ** -0.5)
    sk = jnp.broadcast_to(sink.reshape(1, G, R, 1, 1).astype(s.dtype), s.shape[:-1] + (1,))
    p = jax.nn.softmax(jnp.concatenate([s, sk], axis=-1).astype(jnp.float32), axis=-1).astype(v.dtype)
    return jnp.einsum('bgrqk,bkgd->bqgrd', p[..., :-1], v)


def merge_branches(p, y_a, y_b, y_c, w_o_ssd, w_o_na, w_o_swa, w_out):
    b, L = p.shape[:2]
    g = jax.nn.sigmoid(p[..., GATE_OFF:GATE_OFF + N_BRANCHES * D_MODEL].reshape(b, L, N_BRANCHES, D_MODEL))
    m = g[:, :, 0] * (y_a @ w_o_ssd) + g[:, :, 1] * (y_b @ w_o_na) + g[:, :, 2] * (y_c @ w_o_swa)
    return m @ w_out


def sq_relu_ffn(h, w1, w2):
    return jnp.square(jax.nn.relu(h @ w1)) @ w2


def hybrid_layer(xl, xc, c, c_ctx, ada_w, ada_b, norm1_g, norm2_g, w_in, conv_w, conv_b,
                 dt_bias, a_log, ssd_d, ssd_norm_g, na_rpb, swa_sink, w_o_ssd, w_o_na,
                 w_o_swa, w_out, w_ff1, w_ff2, last):
    b, L = xl.shape[:2]
    n_ctx = xc.shape[1]
    mod_l = jnp.split((jax.nn.silu(c) @ ada_w + ada_b)[:, None, :], N_MOD, axis=-1)
    mod_c = jnp.split((jax.nn.silu(c_ctx) @ ada_w + ada_b)[None, None, :], N_MOD, axis=-1)
    hl = rmsnorm(xl, norm1_g) * (1 + mod_l[1]) + mod_l[0]
    hc = rmsnorm(xc, norm1_g) * (1 + mod_c[1]) + mod_c[0]
    pl = hl @ w_in
    pc = hc @ (w_in[:, :CTX_COLS] if last else w_in)

    A = -jnp.exp(a_log.astype(jnp.float32)).reshape(2, SSD_GROUPS, SSD_HPG)
    xs_c, B_c, C_c, dt_c = ssd_inputs(pc, conv_w, conv_b)
    zero = jnp.zeros((b, SSD_GROUPS, SSD_HPG, SSD_HEAD_DIM, SSD_STATE), jnp.float32)
    ya_c, s_f, s_b = ssd_bidirectional(xs_c, B_c, C_c, dt_c, dt_bias, A, zero, zero, not last)
    xs_l, B_l, C_l, dt_l = ssd_inputs(pl, conv_w, conv_b)
    ya_l, _, _ = ssd_bidirectional(xs_l, B_l, C_l, dt_l, dt_bias, A, s_f, s_b, True)
    ya_l = ssd_output(ya_l, xs_l, ssd_d, pl[..., Z_OFF:Z_OFF + SSD_D_INNER], ssd_norm_g)

    na_kc = heads(pc, NA_K_OFF, NA_HEADS, NA_HEAD_DIM)
    na_vc = heads(pc, NA_V_OFF, NA_HEADS, NA_HEAD_DIM)
    yb_l = na_latent(heads(pl, NA_Q_OFF, NA_HEADS, NA_HEAD_DIM), heads(pl, NA_K_OFF, NA_HEADS, NA_HEAD_DIM),
                     heads(pl, NA_V_OFF, NA_HEADS, NA_HEAD_DIM), na_kc, na_vc, na_rpb).reshape(b, L, NA_WIDTH)

    swa_kc = heads(pc, SWA_K_OFF, SWA_KV_HEADS, SWA_HEAD_DIM)
    swa_vc = heads(pc, SWA_V_OFF, SWA_KV_HEADS, SWA_HEAD_DIM)
    swa_q = axial_rope(heads(pl, SWA_Q_OFF, SWA_HEADS, SWA_HEAD_DIM)).reshape(b, L, SWA_KV_HEADS, SWA_GQA, SWA_HEAD_DIM)
    swa_k = axial_rope(heads(pl, SWA_K_OFF, SWA_KV_HEADS, SWA_HEAD_DIM))
    yc_l = swa_latent(swa_q, swa_k, heads(pl, SWA_V_OFF, SWA_KV_HEADS, SWA_HEAD_DIM),
                      swa_kc, swa_vc, swa_sink).reshape(b, L, SWA_Q_WIDTH)

    xl = xl + mod_l[2] * merge_branches(pl, ya_l, yb_l, yc_l, w_o_ssd, w_o_na, w_o_swa, w_out)
    xl = xl + mod_l[5] * sq_relu_ffn(rmsnorm(xl, norm2_g) * (1 + mod_l[4]) + mod_l[3], w_ff1, w_ff2)

    if not last:
        ya_c = ssd_output(ya_c, xs_c, ssd_d, pc[..., Z_OFF:Z_OFF + SSD_D_INNER], ssd_norm_g)
        yb_c = ctx_mha(heads(pc, NA_Q_OFF, NA_HEADS, NA_HEAD_DIM), na_kc, na_vc).reshape(b, n_ctx, NA_WIDTH)
        q_c = heads(pc, SWA_Q_OFF, SWA_HEADS, SWA_HEAD_DIM).reshape(b, n_ctx, SWA_KV_HEADS, SWA_GQA, SWA_HEAD_DIM)
        yc_c = ctx_swa_sink(q_c, swa_kc, swa_vc, swa_sink).reshape(b, n_ctx, SWA_Q_WIDTH)
        xc = xc + mod_c[2] * merge_branches(pc, ya_c, yb_c, yc_c, w_o_ssd, w_o_na, w_o_swa, w_out)
        xc = xc + mod_c[5] * sq_relu_ffn(rmsnorm(xc, norm2_g) * (1 + mod_c[4]) + mod_c[3], w_ff1, w_ff2)
    return xl, xc


def setup_inputs(seed: int = 0) -> dict:
    key = jax.random.key(seed)
    ks = jax.random.split(key, 26)
    f32 = jnp.float32
    nrm = lambda k, shape, s: jax.random.normal(k, shape, f32) * s
    dt0 = jnp.exp(jax.random.uniform(ks[11], (DEPTH, 2, SSD_HEADS), f32)
                  * (math.log(0.1) - math.log(0.001)) + math.log(0.001))
    return {
        "x": nrm(ks[0], (BATCH, SEQ, D_MODEL), 1.0),
        "c": nrm(ks[1], (BATCH, D_MODEL), 1.0),
        "ctx": nrm(ks[2], (BATCH, CTX_LEN, D_MODEL), 1.0),
        "c_ctx": nrm(ks[3], (D_MODEL,), 1.0),
        "ada_w": nrm(ks[4], (DEPTH, D_MODEL, N_MOD * D_MODEL), 0.5 * D_MODEL ** -0.5),
        "ada_b": nrm(ks[5], (DEPTH, N_MOD * D_MODEL), 0.02),
        "norm1_g": 1.0 + nrm(ks[6], (DEPTH, D_MODEL), 0.05),
        "norm2_g": 1.0 + nrm(ks[7], (DEPTH, D_MODEL), 0.05),
        "w_in": nrm(ks[8], (DEPTH, D_MODEL, IN_COLS), D_MODEL ** -0.5),
        "conv_w": nrm(ks[9], (DEPTH, SSD_CONV, SSD_XBC), SSD_CONV ** -0.5),
        "conv_b": nrm(ks[10], (DEPTH, SSD_XBC), 0.02),
        "dt_bias": dt0 + jnp.log(-jnp.expm1(-dt0)),
        "a_log": jnp.log(jax.random.uniform(ks[12], (DEPTH, 2, SSD_HEADS), f32, 1.0, 16.0)),
        "ssd_d": 1.0 + nrm(ks[13], (DEPTH, SSD_HEADS), 0.1),
        "ssd_norm_g": 1.0 + nrm(ks[14], (DEPTH, SSD_D_INNER), 0.05),
        "na_rpb": nrm(ks[15], (DEPTH, NA_HEADS, 2 * NA_MAX_KH - 1, 2 * NA_KW - 1), 0.1),
        "swa_sink": nrm(ks[16], (DEPTH, SWA_HEADS), 0.5),
        "w_o_ssd": nrm(ks[17], (DEPTH, SSD_D_INNER, D_MODEL), SSD_D_INNER ** -0.5),
        "w_o_na": nrm(ks[18], (DEPTH, NA_WIDTH, D_MODEL), NA_WIDTH ** -0.5),
        "w_o_swa": nrm(ks[19], (DEPTH, SWA_Q_WIDTH, D_MODEL), SWA_Q_WIDTH ** -0.5),
        "w_out": nrm(ks[20], (DEPTH, D_MODEL, D_MODEL), D_MODEL ** -0.5),
        "w_ff1": nrm(ks[21], (DEPTH, D_MODEL, D_FF), D_MODEL ** -0.5),
        "w_ff2": nrm(ks[22], (DEPTH, D_FF, D_MODEL), D_FF ** -0.5),
        "final_g": 1.0 + nrm(ks[23], (D_MODEL,), 0.05),
    }


def reference(x, c, ctx, c_ctx, ada_w, ada_b, norm1_g, norm2_g, w_in, conv_w, conv_b, dt_bias,
              a_log, ssd_d, ssd_norm_g, na_rpb, swa_sink, w_o_ssd, w_o_na, w_o_swa, w_out,
              w_ff1, w_ff2, final_g):
    xl, xc = x, ctx
    for l in range(DEPTH):
        xl, xc = hybrid_layer(xl, xc, c, c_ctx, ada_w[l], ada_b[l], norm1_g[l], norm2_g[l], w_in[l],
                              conv_w[l], conv_b[l], dt_bias[l], a_log[l], ssd_d[l], ssd_norm_g[l],
                              na_rpb[l], swa_sink[l], w_o_ssd[l], w_o_na[l], w_o_swa[l], w_out[l],
                              w_ff1[l], w_ff2[l], last=(l == DEPTH - 1))
    return rmsnorm(xl, final_g)
```

```python
import numpy as np
import concourse.bass as bass
import concourse.mybir as mybir
from concourse.bass_utils import run_bass_kernel_spmd
from contextlib import ExitStack
import types


def _freeze(fn):
    if fn.__closure__:
        cells = []
        for c in fn.__closure__:
            try:
                cells.append(types.CellType(c.cell_contents))
            except ValueError:
                cells.append(c)
        return types.FunctionType(fn.__code__, fn.__globals__, fn.__name__, fn.__defaults__, tuple(cells))
    return fn

F32 = mybir.dt.float32
BF16 = mybir.dt.bfloat16
AF = mybir.ActivationFunctionType
ALU = mybir.AluOpType
AX = mybir.AxisListType

D = 1024; L = 2048; NCTX = 256; T = 2304; NT = 18; KC = 8; DEPTH = 4
IN_COLS = 13888
XBC_OFF = 0; DT_OFF = 4096; NA_K_OFF = 4160; NA_V_OFF = 5184; SWA_K_OFF = 6208; SWA_V_OFF = 6464
Z_OFF = 6720; NA_Q_OFF = 8768; SWA_Q_OFF = 9792; GATE_OFF = 10816
EPS = 1e-6
NEG = -30000.0
C_ID, C_MLE, C_NMLT, C_U, C_LW, C_ONES, C_NMF, C_NMB, C_MF01, C_MB01, C_SWLO, C_SWHI, C_NMLE = range(13)
NCST = 13
BLKS = [(0, 256)] + [(256 + 512 * i, 512) for i in range(4)]


class Sched:
    CE = ('pe', 'act', 'dve', 'pool')
    ALLQ = ('pe', 'act', 'dve', 'pool', 'sp')

    def __init__(self, nc, es):
        self.nc = nc
        self.es = es
        self.ops = {e: [] for e in self.ALLQ}
        self.sem = {e: es.enter_context(nc.semaphore('c_' + e)) for e in self.CE}
        self.cnt = {e: 0 for e in self.CE}
        self.seen = {e: {} for e in self.ALLQ}
        self.buf = {}
        self.dsem = {}
        self.dcnt = {}
        self.semobj = {}
        for e in self.CE:
            self.semobj[id(self.sem[e])] = self.sem[e]
        self.nwaits = 0

    def _b(self, k):
        b = self.buf.get(k)
        if b is None:
            b = self.buf[k] = {'w': {}, 'r': {}}
        return b

    def _emit_waits(self, q, toks):
        need = {}
        for t in toks:
            if t is None:
                continue
            sid, v = t
            if self.seen[q].get(sid, 0) >= v:
                continue
            if need.get(sid, 0) < v:
                need[sid] = v
        for sid, v in need.items():
            self.seen[q][sid] = v
            s = self.semobj[sid]
            self.ops[q].append(lambda E, s=s, v=v: E.wait_ge(s, v))
            self.nwaits += 1

    def _deps(self, q, r, w):
        toks = []
        own = id(self.sem[q]) if q == 'pe' else None
        for k in r:
            b = self._b(k)
            toks.extend(b['w'].items())
        for k in w:
            b = self._b(k)
            toks.extend(b['w'].items())
            toks.extend(b['r'].items())
        toks = [t for t in toks if t is not None and t[0] != own]
        self._emit_waits(q, toks)

    def _commit(self, tok, r, w, acc=False):
        for k in r:
            b = self._b(k)
            if b['r'].get(tok[0], 0) < tok[1]:
                b['r'][tok[0]] = tok[1]
        for k in w:
            if acc:
                b = self._b(k)
                b['w'][tok[0]] = max(b['w'].get(tok[0], 0), tok[1])
            else:
                self.buf[k] = {'w': {tok[0]: tok[1]}, 'r': {}}

    def op(self, q, fn, r=(), w=()):
        fn = _freeze(fn)
        self._deps(q, r, w)
        self.cnt[q] += 1
        s = self.sem[q]
        self.ops[q].append(lambda E, fn=fn, s=s: fn(E).then_inc(s, 1))
        self._commit((id(s), self.cnt[q]), r, w)

    def dma(self, q, out, in_, r=(), w=(), semkey=None, acc=False, **kw):
        if acc:
            self._deps(q, r, ())
            toks = []
            for k in w:
                toks.extend(self._b(k)['r'].items())
            self._emit_waits(q, toks)
        else:
            self._deps(q, r, w)
        k0 = semkey if semkey is not None else w[0]
        if k0 not in self.dsem:
            s = self.es.enter_context(self.nc.semaphore('d%d' % len(self.dsem)))
            self.dsem[k0] = s
            self.dcnt[k0] = 0
            self.semobj[id(s)] = s
        s = self.dsem[k0]
        self.dcnt[k0] += 16
        self.ops[q].append(lambda E, s=s, out=out, in_=in_, kw=kw: E.dma_start(out=out, in_=in_, **kw).then_inc(s, 16))
        self._commit((id(s), self.dcnt[k0]), r, w, acc)

    def alias_sem(self, knew, kold):
        raise NotImplementedError

    def barrier(self):
        toks = [(id(self.sem[e]), self.cnt[e]) for e in self.CE]
        toks += [(id(self.dsem[k]), self.dcnt[k]) for k in self.dsem]
        for q in self.ALLQ:
            own = id(self.sem[q]) if q in self.sem else None
            self._emit_waits(q, [t for t in toks if t[0] != own and t[1] > 0])
        self.buf = {}

    def finish(self):
        self.barrier()
        nc = self.nc
        ops = self.ops
        block = self.es.enter_context(nc.Block())

        @block.tensor
        def _(E):
            for f in ops['pe']:
                f(E)

        @block.scalar
        def _(E):
            for f in ops['act']:
                f(E)

        @block.vector
        def _(E):
            for f in ops['dve']:
                f(E)

        @block.gpsimd
        def _(E):
            for f in ops['pool']:
                f(E)

        @block.sync
        def _(E):
            for f in ops['sp']:
                f(E)


def na_key_tiles(t):
    rows = 32
    qr = [2 * t, 2 * t + 1]
    rs = [int(np.clip(r - 4, 0, rows - 8)) for r in qr]
    lo = min(rs); hi = max(rs) + 8
    us = list(range(lo // 2, (hi + 1) // 2))
    return us


def build_na_tables(rpb):
    rows = 32
    ids = {}
    pats = []
    for t in range(16):
        for u in na_key_tiles(t):
            qr = np.array([2 * t, 2 * t + 1])
            kr = np.array([2 * u, 2 * u + 1])
            rs = np.clip(qr - 4, 0, rows - 8)
            rvalid = (kr[:, None] >= rs[None, :]) & (kr[:, None] < rs[None, :] + 8)
            ridx = kr[:, None] - qr[None, :] + 7
            key = (tuple(rvalid.ravel().tolist()), tuple(ridx.ravel().tolist()))
            if key not in pats:
                pats.append(key)
            ids[(t, u)] = pats.index(key)
    qc = np.arange(64); kc = np.arange(64)
    cs = np.clip(qc - 8, 0, 64 - 16)
    cvalid = (kc[:, None] >= cs[None, :]) & (kc[:, None] < cs[None, :] + 16)
    cidx = np.clip(kc[:, None] - qc[None, :] + 15, 0, 30)
    tabs = np.full((16, len(pats), 128, 128), NEG, np.float32)
    for pi, (rv, ri) in enumerate(pats):
        rv = np.array(rv).reshape(2, 2); ri = np.array(ri).reshape(2, 2)
        for krl in range(2):
            for qrl in range(2):
                if not rv[krl, qrl]:
                    continue
                g = rpb[:, int(np.clip(ri[krl, qrl], 0, 14)), :][:, cidx]
                blk = np.where(cvalid[None], g, np.float32(NEG))
                tabs[:, pi, krl * 64:(krl + 1) * 64, qrl * 64:(qrl + 1) * 64] = blk
    return tabs, ids


_NA_IDS = None


def na_ids():
    global _NA_IDS
    if _NA_IDS is None:
        _, _NA_IDS = build_na_tables(np.zeros((16, 15, 31), np.float32))
    return _NA_IDS


NB_NA = 9


def host_consts():
    i = np.arange(128)
    k = i[:, None]; j = i[None, :]
    c = np.zeros((NCST, 128, 128), np.float32)
    c[C_ID] = (k == j)
    c[C_MLE] = (k <= j)
    c[C_NMLT] = -(k < j).astype(np.float32)
    c[C_U] = (k > j)
    c[C_LW] = (k < j)
    c[C_ONES] = 1.0
    c[C_NMF] = NEG * (k > j)
    c[C_NMB] = NEG * (k < j)
    c[C_MF01] = (k <= j)
    c[C_MB01] = (k >= j)
    c[C_SWLO] = NEG * (k < j)
    c[C_SWHI] = NEG * (k > j)
    c[C_NMLE] = -(k <= j).astype(np.float32)
    return c


def host_rope():
    pos = np.arange(L)
    inv = (10000.0 ** (-np.arange(16, dtype=np.float32) / 16)).astype(np.float32)
    cos = np.zeros((64, L), np.float32); sin = np.zeros((64, L), np.float32)
    for half, p in enumerate([pos // 64, pos % 64]):
        ang = p.astype(np.float32)[None, :] * inv[:, None]
        cs = np.cos(ang).astype(np.float32); sn = np.sin(ang).astype(np.float32)
        b = half * 32
        cos[b:b + 16] = cs; cos[b + 16:b + 32] = cs
        sin[b:b + 16] = -sn; sin[b + 16:b + 32] = sn
    return np.concatenate([cos, cos], 0), np.concatenate([sin, sin], 0)


def build(nc, n_layers=DEPTH, dbg=False, stop_after=None):
    es = ExitStack()
    S = Sched(nc, es)
    sk = "ExternalOutput" if dbg else "Internal"

    def din(name, shape, dt=F32):
        return nc.dram_tensor(name, list(shape), dt, kind="ExternalInput").ap()

    def dscr(name, shape, dt=BF16):
        return nc.dram_tensor(name, list(shape), dt, kind=sk).ap()

    x_in = din("x", [L, D]); ctx_in = din("ctx", [NCTX, D]); cc_in = din("cc", [128, KC, 2])
    ada_w = din("ada_w", [DEPTH, D, 6 * D]); ada_b2 = din("ada_b2", [DEPTH, 2, 6 * D])
    n1g_in = din("n1g", [DEPTH, 128, KC]); n2g_in = din("n2g", [DEPTH, 128, KC]); fing_in = din("final_g", [1, D])
    w_in = din("w_in", [DEPTH, D, IN_COLS])
    cw_in = din("cw", [DEPTH, 128, 32, 5]); cb_in = din("cb", [DEPTH, 128, 32])
    dtb_in = din("dtb", [DEPTH, 1, 64]); alog_in = din("alog", [DEPTH, 1, 64]); ssdd_in = din("ssdd", [DEPTH, 1, 32])
    sng_in = din("sng", [DEPTH, 128, 16])
    nab_in = din("nab", [DEPTH, 16, NB_NA, 128, 128]); sink_in = din("sink", [DEPTH, 1, 16])
    w_o_ssd = din("w_o_ssd", [DEPTH, 2048, D]); w_o_na = din("w_o_na", [DEPTH, D, D]); w_o_swa = din("w_o_swa", [DEPTH, D, D])
    w_out = din("w_out", [DEPTH, D, D]); w_ff1 = din("w_ff1", [DEPTH, D, 4 * D]); w_ff2 = din("w_ff2", [DEPTH, 4 * D, D])
    cst_in = din("cst", [NCST, 128, 128]); ropec_in = din("ropec", [128, L]); ropes_in = din("ropes", [128, L])
    out_d = nc.dram_tensor("out", [L, D], F32, kind="ExternalOutput").ap()

    xres = dscr("xres", [T, D], F32)
    XBM = dscr("XBM", [T, 3072]); BCT = dscr("BCT", [2048, T]); ZTM = dscr("ZTM", [T, 2048])
    NAK = dscr("NAK", [D, T]); NAQ = dscr("NAQ", [D, T]); NAV = dscr("NAV", [T, D])
    SWK = dscr("SWK", [256, T]); SWQ = dscr("SWQ", [D, T]); SWV = dscr("SWV", [T, 256])
    GFM = dscr("GFM", [3072, T]); SBST = dscr("SBST", [NT, 128, 2048])

    uid = [0]

    def sb(name, shape, dt=F32, stack=None):
        uid[0] += 1
        return (stack or es).enter_context(nc.sbuf_tensor('s%d_%s' % (uid[0], name), list(shape), dt))

    psall = es.enter_context(nc.psum_tensor("psall", [128, 4096], F32))

    def PS(b, off=0, n=512):
        return psall[:, b * 512 + off:b * 512 + off + n]

    def PSB(b):
        return psall[:, b * 512:(b + 1) * 512].bitcast(BF16)

    def PK(*banks):
        ks = []
        for b in banks:
            ks.append('ps%d' % b)
            if b < 2:
                ks += ['pe%d' % r for r in range(4 * b, 4 * b + 4)]
        return ks

    def bc3(ap, n):
        return ap.unsqueeze(2).broadcast_to([ap.shape[0], ap.shape[1], n])

    def v3(ap, d):
        return ap.rearrange("p (h d) -> p h d", d=d)

    cst = sb("cst", [128, NCST, 128], F32)
    cstb = sb("cstb", [128, NCST, 128], BF16)
    identf = cst[:, C_ID, :]
    identb = cstb[:, C_ID, :]
    cc = sb("ccs", [128, KC, 2], F32)
    sc2 = sb("sc2", [128, KC, 2], BF16)
    fing = sb("fing", [128, D], F32)
    modT = sb("modT", [128, 48, 2], F32)
    G1 = sb("G1", [128, KC, 2], F32); S1 = sb("S1", [128, KC, 2], F32)
    G2 = sb("G2", [128, KC, 2], F32); S2 = sb("S2", [128, KC, 2], F32)
    gbc = sb("gbc", [128, 2, 2, D], F32)
    sel = sb("sel", [2, 2, 128], F32)
    ngam = sb("ngam", [128, 2, KC], F32)
    dt_all = sb("dt_all", [128, NT, 64], F32)
    A2 = sb("A2", [128, 64], F32); dtb2 = sb("dtb2", [128, 64], F32); Dbc = sb("Dbc", [128, 32], F32)
    sinkexp = sb("sinkexp", [128, 16], F32)
    epsb = sb("epsb", [128, 1], F32)

    q_sp = 'sp'
    q_cast = 'pool'

    for i in range(NCST):
        S.dma(q_sp, cst[:, i, :], cst_in[i], w=['cst'])
    S.op('act', lambda E: E.activation(out=cstb[:], in_=cst[:], func=AF.Copy), r=['cst'], w=['cstb'])
    S.dma(q_sp, cc[:], cc_in[:, :, :], w=['cc'])
    S.dma(q_sp, fing[:], fing_in[0:1, :].partition_broadcast(128), w=['fing'])
    S.op('act', lambda E: E.activation(out=sc2[:], in_=cc[:], func=AF.Silu), r=['cc'], w=['sc2'])
    S.op('pool', lambda E: E.memset(sel[:], 0.0), w=['sel'])
    S.op('pool', lambda E: E.memset(sel[0:1, 0, :], 1.0), w=['sel'])
    S.dma(q_sp, sel[1:2, 1, :], cst_in[C_ONES, 0:1, :], r=[], w=['sel'])
    S.op('pool', lambda E: E.memset(epsb[:], EPS), w=['epsb'])

    def tok_type(tt):
        return 1 if tt < 2 else 0

    def xsrc(l, tt):
        if l == 0:
            return ctx_in[tt * 128:(tt + 1) * 128, :] if tt < 2 else x_in[(tt - 2) * 128:(tt - 1) * 128, :]
        return xres[tt * 128:(tt + 1) * 128, :]

    def cast_load(dst, src, key, rows_per=None):
        S.dma(q_cast, dst, src, w=[key])

    def norm_to_T(l, tt, xt, xkey, G, Sh, hT, hkey, col0, wk):
        junk, ss, xn = wk['junk'], wk['ss'], wk['xn']
        tk = tok_type(tt)
        S.op('act', lambda E: E.activation(out=junk[:], in_=xt, func=AF.Square, accum_out=ss[:]), r=[xkey], w=['junk', 'ss'])
        S.op('dve', lambda E: E.tensor_scalar(out=ss[:], in0=ss[:], scalar1=1.0 / D, scalar2=EPS, op0=ALU.mult, op1=ALU.add), r=['ss'], w=['ss'])
        S.op('act', lambda E: E.activation(out=ss[:], in_=ss[:], func=AF.Sqrt), r=['ss'], w=['ss'])
        S.op('dve', lambda E: E.reciprocal(out=ss[:], in_=ss[:]), r=['ss'], w=['ss'])
        S.op('dve', lambda E: E.tensor_scalar(out=xn[:], in0=xt, scalar1=ss[:, 0:1], scalar2=None, op0=ALU.mult), r=[xkey, 'ss'], w=['xn'])
        pb = PSB(5)
        for k in range(KC):
            S.op('pe', lambda E, k=k: E.transpose(out=pb[:, k * 128:(k + 1) * 128], in_=xn[:, k * 128:(k + 1) * 128], identity=identb),
                 r=['xn', 'cstb'], w=['ps5'])
        for k in range(KC):
            S.op('act', lambda E, k=k: E.activation(out=hT[:, k, col0:col0 + 128], in_=pb[:, k * 128:(k + 1) * 128], func=AF.Identity,
                                                     scale=G[:, k, tk:tk + 1], bias=Sh[:, k, tk:tk + 1]),
                 r=['ps5', 'GS'], w=[hkey])

    def phase_adaln(l):
        with ExitStack() as ps_:
            wa = [sb("adaw%d" % i, [128, KC, 512], BF16, ps_) for i in range(2)]
            m2 = sb("m2", [2, 512], F32, ps_)
            ab = sb("ab", [2, 6 * D], F32, ps_)
            n1 = sb("n1", [128, KC], F32, ps_); n2 = sb("n2", [128, KC], F32, ps_)
            S.dma(q_sp, ab[:], ada_b2[l], w=['ab'])
            S.dma(q_sp, n1[:], n1g_in[l], w=['n1']); S.dma(q_sp, n2[:], n2g_in[l], w=['n2'])
            for g in range(12):
                w = wa[g % 2]; wk = 'adaw%d' % (g % 2)
                cast_load(w[:], ada_w[l, :, g * 512:(g + 1) * 512].rearrange("(k p) n -> p k n", p=128), wk)
                for k in range(KC):
                    S.op('pe', lambda E, k=k, w=w: E.matmul(PS(0)[0:2, :], lhsT=sc2[:, k, :], rhs=w[:, k, :], start=(k == 0), stop=(k == KC - 1)),
                         r=['sc2', wk], w=['ps0'])
                S.op('dve', lambda E, g=g: E.tensor_tensor(out=m2[:], in0=PS(0)[0:2, :], in1=ab[:, g * 512:(g + 1) * 512], op=ALU.add),
                     r=['ps0', 'ab'], w=['m2'])
                for j in range(4):
                    jj = g * 4 + j
                    S.op('pe', lambda E, j=j, jj=jj: E.transpose(out=PS(1)[:, jj * 2:jj * 2 + 2], in_=m2[0:2, j * 128:(j + 1) * 128], identity=identf[0:2, 0:2]),
                         r=['m2', 'cst'], w=['ps1'])
                if g in (4, 5, 10, 11):
                    gi = 0 if g < 6 else 1
                    half = g % 2
                    for t in range(2):
                        S.op('pe', lambda E, t=t: E.matmul(PS(2 + t), lhsT=sel[0:2, t, :], rhs=m2[0:2, :], start=True, stop=True),
                             r=['m2', 'sel'], w=['ps%d' % (2 + t)])
                        S.op('act', lambda E, t=t, gi=gi, half=half: E.activation(out=gbc[:, t, gi, half * 512:(half + 1) * 512], in_=PS(2 + t), func=AF.Copy),
                             r=['ps%d' % (2 + t)], w=['gbc'])
            S.op('dve', lambda E: E.tensor_copy(out=modT[:].rearrange("p j t -> p (j t)"), in_=PS(1)[:, 0:96]), r=['ps1'], w=['modT'])
            for (G, Sh, ng, js, jsh) in ((G1, S1, n1, 8, 0), (G2, S2, n2, 32, 24)):
                for t in range(2):
                    S.op('dve', lambda E, G=G, ng=ng, js=js, t=t: E.scalar_tensor_tensor(out=G[:, :, t], in0=modT[:, js:js + 8, t], scalar=1.0, in1=ng[:],
                                                                                         op0=ALU.add, op1=ALU.mult), r=['modT', 'n1', 'n2'], w=['GS'])
                    S.op('dve', lambda E, Sh=Sh, jsh=jsh, t=t: E.tensor_copy(out=Sh[:, :, t], in_=modT[:, jsh:jsh + 8, t]), r=['modT'], w=['GS'])
            S.dma(q_sp, dtb2[:], dtb_in[l].partition_broadcast(128), w=['dtb2'])
            S.dma(q_sp, A2[:], alog_in[l].partition_broadcast(128), w=['A2'])
            S.dma(q_sp, Dbc[:], ssdd_in[l].partition_broadcast(128), w=['Dbc'])
            S.dma(q_sp, sinkexp[:], sink_in[l].partition_broadcast(128), w=['sinkexp'])
            S.op('act', lambda E: E.activation(out=A2[:], in_=A2[:], func=AF.Exp), r=['A2'], w=['A2'])
            S.op('dve', lambda E: E.tensor_scalar(out=A2[:], in0=A2[:], scalar1=-1.0, scalar2=None, op0=ALU.mult), r=['A2'], w=['A2'])
            S.op('act', lambda E: E.activation(out=sinkexp[:], in_=sinkexp[:], func=AF.Exp), r=['sinkexp'], w=['sinkexp'])
            S.barrier()

    def phase_proj(l):
        last = (l == n_layers - 1) and (n_layers == DEPTH)
        with ExitStack() as ps_:
            hT = sb("hT", [128, KC, T], BF16, ps_)
            wk = dict(junk=sb("junk", [128, D], BF16, ps_), ss=sb("ss", [128, 1], F32, ps_), xn=sb("xn", [128, D], BF16, ps_))
            xt2 = [sb("xt%d" % i, [128, D], F32, ps_) for i in range(2)]
            for tt in range(NT):
                xt = xt2[tt % 2]; xk = 'xt%d' % (tt % 2)
                S.dma(q_sp, xt[:], xsrc(l, tt), w=[xk])
                if l == 0:
                    S.dma(q_sp, xres[tt * 128:(tt + 1) * 128, :], xt[:], r=[xk], w=['xres'])
                norm_to_T(l, tt, xt[:], xk, G1, S1, hT, 'hT', tt * 128, wk)
            wb = [sb("wb%d" % i, [128, KC, 512], BF16, ps_) for i in range(3)]
            wsw = sb("wsw", [128, KC, 512], BF16, ps_)
            stg = [sb("stg%d" % i, [128, 512], BF16, ps_) for i in range(4)]
            ropec = sb("ropec", [128, L], F32, ps_); ropes = sb("ropes", [128, L], F32, ps_)
            rt1 = sb("rt1", [128, 512], F32, ps_); rt2 = sb("rt2", [128, 512], F32, ps_)
            cw = sb("cw", [128, 32, 5], F32, ps_); cbv = sb("cbv", [128, 32], F32, ps_)
            dtr = sb("dtr", [128, 64], F32, ps_)
            conv_scope = ExitStack()
            cvins = [sb("cvin%d" % i, [128, 2312], F32, conv_scope) for i in range(2)]; cvaccs = [sb("cvacc%d" % i, [128, 2308], F32, conv_scope) for i in range(2)]
            cvos = [sb("cvo%d" % i, [128, 2308], BF16, conv_scope) for i in range(2)]; tmst = sb("tmst", [128, NT, 128], BF16, conv_scope)
            S.dma(q_sp, ropec[:], ropec_in[:, :], w=['ropec']); S.dma(q_sp, ropes[:], ropes_in[:, :], w=['ropes'])
            S.dma(q_sp, cw[:], cw_in[l], w=['cw']); S.dma(q_sp, cbv[:], cb_in[l], w=['cbv'])
            for i in range(2):
                S.op('pool', lambda E, i=i: E.memset(cvins[i][:], 0.0), w=['cvin%d' % i])
            state = {'wi': 0, 'si': 0, 'pi': 0}

            def load_w(c0, ncols):
                i = state['wi'] % 3; state['wi'] += 1
                w = wb[i]
                cast_load(w[:, :, 0:ncols], w_in[l, :, c0:c0 + ncols].rearrange("(k p) n -> p k n", p=128), 'wb%d' % i)
                return w, 'wb%d' % i

            def nbank():
                b = state['pi'] % 4; state['pi'] += 1
                return b

            def nstg():
                i = state['si'] % 4; state['si'] += 1
                return stg[i], 'stg%d' % i

            def mm_fm(w, wkey, j, blk, bank):
                t0, n = BLKS[blk]
                for k in range(KC):
                    S.op('pe', lambda E, k=k: E.matmul(PS(bank, 0, n), lhsT=w[:, k, j * 128:(j + 1) * 128], rhs=hT[:, k, t0:t0 + n],
                                                       start=(k == 0), stop=(k == KC - 1)), r=[wkey, 'hT'], w=['ps%d' % bank])

            def mm_tm(w, wkey, ncols, tt, bank):
                for k in range(KC):
                    S.op('pe', lambda E, k=k: E.matmul(PS(bank, 0, ncols), lhsT=hT[:, k, tt * 128:(tt + 1) * 128], rhs=w[:, k, 0:ncols],
                                                       start=(k == 0), stop=(k == KC - 1)), r=[wkey, 'hT'], w=['ps%d' % bank])

            def fm_simple(c0, ngroups_cols, dst, dkey, func, scale=1.0, skip_ctx=False):
                ncols = ngroups_cols
                for g0 in range(0, ncols, 512):
                    nc_ = min(512, ncols - g0)
                    w, wkey = load_w(c0 + g0, nc_)
                    for j in range(nc_ // 128):
                        row0 = g0 + j * 128
                        for blk in range(5):
                            if skip_ctx and blk == 0:
                                continue
                            t0, n = BLKS[blk]
                            bank = nbank()
                            mm_fm(w, wkey, j, blk, bank)
                            st, sk_ = nstg()
                            S.op('act', lambda E, st=st, bank=bank, n=n: E.activation(out=st[:, 0:n], in_=PS(bank, 0, n), func=func, scale=scale),
                                 r=['ps%d' % bank], w=[sk_])
                            S.dma(q_sp, dst[row0:row0 + 128, t0:t0 + n], st[:, 0:n], r=[sk_], w=[dkey], semkey='o' + sk_, acc=True)
                        yield

            def tm_simple(c0, ncols, dst, dkey, func):
                for g0 in range(0, ncols, 512):
                    nc_ = min(512, ncols - g0)
                    w, wkey = load_w(c0 + g0, nc_)
                    for tt in range(NT):
                        bank = nbank()
                        mm_tm(w, wkey, nc_, tt, bank)
                        st, sk_ = nstg()
                        S.op('act', lambda E, st=st, bank=bank: E.activation(out=st[:, 0:nc_], in_=PS(bank, 0, nc_), func=func), r=['ps%d' % bank], w=[sk_])
                        S.dma(q_sp, dst[tt * 128:(tt + 1) * 128, g0:g0 + nc_], st[:, 0:nc_], r=[sk_], w=[dkey], semkey='o' + sk_, acc=True)
                        if tt % 4 == 3:
                            yield

            def fm_rope(c0, ncols, dst, dkey, scale):
                for g0 in range(0, ncols, 512):
                    nc_ = min(512, ncols - g0)
                    w, wkey = load_w(c0 + g0, nc_)
                    wv = w[:, :, 0:nc_].rearrange("p k (j two s) -> p k j two s", two=2, s=16)
                    sv = wsw[:, :, 0:nc_].rearrange("p k (j two s) -> p k j two s", two=2, s=16)
                    for k in range(KC):
                        S.op('pool', lambda E, k=k: E.tensor_copy(out=sv[:, k, :, 0, :], in_=wv[:, k, :, 1, :]), r=[wkey], w=['wsw'])
                        S.op('pool', lambda E, k=k: E.tensor_copy(out=sv[:, k, :, 1, :], in_=wv[:, k, :, 0, :]), r=[wkey], w=['wsw'])
                    for j in range(nc_ // 128):
                        row0 = g0 + j * 128
                        for blk in range(5):
                            t0, n = BLKS[blk]
                            bank = nbank()
                            mm_fm(w, wkey, j, blk, bank)
                            st, sk_ = nstg()
                            if blk == 0:
                                S.op('act', lambda E, st=st, bank=bank, n=n: E.activation(out=st[:, 0:n], in_=PS(bank, 0, n), func=AF.Copy, scale=scale),
                                     r=['ps%d' % bank], w=[sk_])
                            else:
                                bank2 = nbank()
                                mm_fm(wsw, 'wsw', j, blk, bank2)
                                l0 = t0 - 256
                                S.op('dve', lambda E, bank=bank, l0=l0: E.tensor_tensor(out=rt1[:], in0=PS(bank), in1=ropec[:, l0:l0 + 512], op=ALU.mult),
                                     r=['ps%d' % bank, 'ropec'], w=['rt1'])
                                S.op('dve', lambda E, bank2=bank2, l0=l0: E.tensor_tensor(out=rt2[:], in0=PS(bank2), in1=ropes[:, l0:l0 + 512], op=ALU.mult),
                                     r=['ps%d' % bank2, 'ropes'], w=['rt2'])
                                S.op('pool', lambda E: E.tensor_tensor(out=rt1[:], in0=rt1[:], in1=rt2[:], op=ALU.add), r=['rt1', 'rt2'], w=['rt1'])
                                S.op('act', lambda E, st=st: E.activation(out=st[:], in_=rt1[:], func=AF.Copy, scale=scale), r=['rt1'], w=[sk_])
                            S.dma(q_sp, dst[row0:row0 + 128, t0:t0 + n], st[:, 0:n], r=[sk_], w=[dkey], semkey='o' + sk_, acc=True)
                        yield

            def conv_post(ct):
                cp = ct % 2
                cvo = cvos[cp]; ko = 'cvo%d' % cp
                if ct >= 16:
                    r0 = (ct - 16) * 128
                    S.dma(q_sp, BCT[r0:r0 + 128, 0:256], cvo[:, 0:256], r=[ko], w=['BCT'], semkey='oBCT%d' % cp, acc=True)
                    S.dma(q_sp, BCT[r0:r0 + 128, 256:T], cvo[:, 260:2308], r=[ko], w=['BCT'], semkey='oBCT%d' % cp, acc=True)
                if ct < 24:
                    for tt in range(NT):
                        o0 = tt * 128 if tt < 2 else 260 + (tt - 2) * 128
                        bnk = 4 + (tt // 8) % 2 if tt < 16 else 6
                        off = (tt % 8) * 128
                        S.op('pe', lambda E, o0=o0, bnk=bnk, off=off: E.transpose(out=PSB(bnk)[:, off:off + 128], in_=cvo[:, o0:o0 + 128], identity=identb),
                             r=[ko, 'cstb'], w=['ps%d' % bnk])
                    for gi, (bnk, n8) in enumerate(((4, 8), (5, 8), (6, 2))):
                        S.op('dve', lambda E, gi=gi, bnk=bnk, n8=n8: E.tensor_copy(out=tmst[:, gi * 8:gi * 8 + n8, :].rearrange("p a b -> p (a b)"),
                                                                                   in_=PSB(bnk)[:, 0:n8 * 128]), r=['ps%d' % bnk], w=['tmst'])
                    S.dma(q_sp, XBM[:, ct * 128:(ct + 1) * 128].rearrange("(t p) c -> p t c", p=128), tmst[:], r=['tmst'], w=['XBM'])

            for g in range(8):
                w, wkey = load_w(XBC_OFF + g * 512, 512)
                for j in range(4):
                    ct = g * 4 + j
                    cp = ct % 2
                    cvin = cvins[cp]; cvacc = cvaccs[cp]; cvo = cvos[cp]
                    kin = 'cvin%d' % cp; kacc = 'cvacc%d' % cp; ko = 'cvo%d' % cp
                    for blk in range(5):
                        t0, n = BLKS[blk]
                        bank = nbank()
                        mm_fm(w, wkey, j, blk, bank)
                        o0 = 2 if blk == 0 else 262 + (t0 - 256)
                        S.op('act', lambda E, bank=bank, n=n, o0=o0: E.activation(out=cvin[:, o0:o0 + n], in_=PS(bank, 0, n), func=AF.Copy),
                             r=['ps%d' % bank], w=[kin])
                    if ct > 0:
                        conv_post(ct - 1)
                    S.op('dve', lambda E, ct=ct: E.tensor_scalar(out=cvacc[:], in0=cvin[:, 0:2308], scalar1=cw[:, ct, 0:1], scalar2=None, op0=ALU.mult),
                         r=[kin, 'cw'], w=[kacc])
                    for kk in range(1, 5):
                        S.op('dve', lambda E, ct=ct, kk=kk: E.scalar_tensor_tensor(out=cvacc[:], in0=cvin[:, kk:kk + 2308], scalar=cw[:, ct, kk:kk + 1],
                                                                                    in1=cvacc[:], op0=ALU.mult, op1=ALU.add), r=[kin, 'cw'], w=[kacc])
                    S.op('act', lambda E, ct=ct: E.activation(out=cvo[:], in_=cvacc[:], func=AF.Silu, bias=cbv[:, ct:ct + 1]), r=[kacc, 'cbv'], w=[ko])
            conv_post(31)
            w, wkey = load_w(DT_OFF, 64)
            for tt in range(NT):
                bank = nbank()
                mm_tm(w, wkey, 64, tt, bank)
                S.op('dve', lambda E, bank=bank: E.tensor_tensor(out=dtr[:], in0=PS(bank, 0, 64), in1=dtb2[:], op=ALU.add), r=['ps%d' % bank, 'dtb2'], w=['dtr'])
                S.op('act', lambda E: E.activation(out=dtr[:], in_=dtr[:], func=AF.Exp), r=['dtr'], w=['dtr'])
                S.op('act', lambda E, tt=tt: E.activation(out=dt_all[:, tt, :], in_=dtr[:], func=AF.Ln, bias=1.0), r=['dtr'], w=['dt_all'])
            S.barrier()
            conv_scope.close()
            bwk = ssd_wk(ps_, '_b'); bwk['sbank'] = 4
            bxB = [sb("bxB%d" % i, [128, 3072], BF16, ps_) for i in range(2)]
            bxw = sb("bxw", [128, 2048], BF16, ps_)
            bst = sb("bst", [128, 2048], F32, ps_); bst_bf = sb("bst_bf", [128, 2048], BF16, ps_); btmp = sb("btmp", [128, 1024], F32, ps_)

            def bwd_gen():
                S.op('pool', lambda E: E.memset(bst[:], 0.0), w=['st'])
                S.op('pool', lambda E: E.memset(bst_bf[:], 0.0), w=['st_bf'])
                order = [1, 0] + list(range(17, 1, -1))
                for i, tt in enumerate(order):
                    xB = bxB[i % 2]; xk = 'bxB%d' % (i % 2)
                    S.dma(q_sp, SBST[tt], bst_bf[:], r=['st_bf'], w=['SBST'], semkey='oSB', acc=True)
                    if i == len(order) - 1:
                        break
                    S.dma(q_sp, xB[:], XBM[tt * 128:(tt + 1) * 128, :], w=[xk])
                    yield
                    yield from ssd_small(tt, bwk)
                    yield
                    yield from ssd_state_update(xB, xk, bxw, 32, 160, bst, bst_bf, btmp, bwk)
                    yield

            def proj_rest():
                yield from fm_simple(NA_K_OFF, 1024, NAK, 'NAK', AF.Copy)
                yield from tm_simple(NA_V_OFF, 1024, NAV, 'NAV', AF.Copy)
                yield from fm_rope(SWA_K_OFF, 256, SWK, 'SWK', 1.0)
                yield from tm_simple(SWA_V_OFF, 256, SWV, 'SWV', AF.Copy)
                yield from tm_simple(Z_OFF, 2048, ZTM, 'ZTM', AF.Silu)
                yield from fm_simple(NA_Q_OFF, 1024, NAQ, 'NAQ', AF.Copy, scale=0.125)
                yield from fm_rope(SWA_Q_OFF, 1024, SWQ, 'SWQ', 0.125)
                yield from fm_simple(GATE_OFF, 3072, GFM, 'GFM', AF.Sigmoid)

            gens = [proj_rest(), bwd_gen()]
            while gens:
                for g_ in list(gens):
                    try:
                        next(g_)
                    except StopIteration:
                        gens.remove(g_)
            S.barrier()


    def ssd_small(tt, wk):
        a2, sg, expS, tmp64, e64, bias64, w64 = (wk[k] for k in ('a2', 'sg', 'expS', 'tmp64', 'e64', 'bias64', 'w64'))
        x = wk.get('sfx', '')
        sbk = wk.get('sbank', 0)
        S.op('dve', lambda E: E.tensor_tensor(out=a2[:], in0=dt_all[:, tt, :], in1=A2[:], op=ALU.mult), r=['dt_all', 'A2'], w=['a2' + x])
        yield
        for i, ci in enumerate((C_U, C_LW, C_ONES)):
            S.op('pe', lambda E, i=i, ci=ci: E.matmul(PS(sbk, i * 64, 64), lhsT=cst[:, ci, :], rhs=a2[:], start=True, stop=True), r=['a2' + x, 'cst'], w=PK(sbk))
        S.op('act', lambda E: E.activation(out=sg[:], in_=PS(sbk, 0, 192), func=AF.Copy), r=PK(sbk), w=['sg' + x])
        S.op('act', lambda E: E.activation(out=expS[:], in_=sg[:], func=AF.Exp), r=['sg' + x], w=['expS' + x])
        S.op('dve', lambda E: E.tensor_tensor(out=tmp64[:, 0:32], in0=sg[:, 128:160], in1=sg[:, 0:32], op=ALU.subtract), r=['sg' + x], w=['tmp64' + x])
        S.op('dve', lambda E: E.tensor_tensor(out=tmp64[:, 32:64], in0=sg[:, 160:192], in1=sg[:, 96:128], op=ALU.subtract), r=['sg' + x], w=['tmp64' + x])
        S.op('act', lambda E: E.activation(out=e64[:], in_=tmp64[:], func=AF.Exp), r=['tmp64' + x], w=['e64' + x])
        S.op('dve', lambda E: E.tensor_tensor(out=w64[:, 0:32], in0=expS[:, 0:32], in1=dt_all[:, tt, 0:32], op=ALU.mult), r=['expS' + x, 'dt_all'], w=['w64' + x])
        S.op('dve', lambda E: E.tensor_tensor(out=w64[:, 32:64], in0=expS[:, 96:128], in1=dt_all[:, tt, 32:64], op=ALU.mult), r=['expS' + x, 'dt_all'], w=['w64' + x])

    def ssd_state_update(xB, xbkey, xw, wcol, cdcol, st, st_bf, tmpY, wk):
        w64, expS = wk['w64'], wk['expS']
        x = wk.get('sfx', '')
        S.op('dve', lambda E: E.tensor_tensor(out=v3(xw[:], 64), in0=v3(xB[:, 0:2048], 64), in1=bc3(w64[:, wcol:wcol + 32], 64), op=ALU.mult),
             r=[xbkey, 'w64' + x], w=['xw'])
        yield
        for half in range(2):
            for gg in range(4):
                g = half * 4 + gg
                S.op('pe', lambda E, g=g, gg=gg: E.matmul(PS(6 + gg // 2, (gg % 2) * 256, 256), lhsT=xB[:, 2048 + g * 128:2048 + (g + 1) * 128],
                                                           rhs=xw[:, g * 256:(g + 1) * 256], start=True, stop=True), r=[xbkey, 'xw'], w=PK(6 + gg // 2))
            c0 = half * 1024
            S.op('pool', lambda E, c0=c0, half=half: E.tensor_tensor(out=v3(tmpY[:, 0:1024], 64), in0=v3(st[:, c0:c0 + 1024], 64),
                                                                    in1=bc3(expS[:, cdcol + half * 16:cdcol + half * 16 + 16], 64), op=ALU.mult),
                 r=['st', 'expS' + x], w=['tmpSU'])
            S.op('dve', lambda E, c0=c0: E.tensor_tensor(out=st[:, c0:c0 + 1024], in0=psall[:, 6 * 512:8 * 512], in1=tmpY[:, 0:1024], op=ALU.add),
                 r=PK(6, 7) + ['tmpSU'], w=['st'])
        S.op('act', lambda E: E.activation(out=st_bf[:], in_=st[:], func=AF.Copy), r=['st'], w=['st_bf'])

    def ssd_wk(ps_, sfx=''):
        return dict(sfx=sfx, a2=sb("a2", [128, 64], F32, ps_), sg=sb("sg", [128, 192], F32, ps_), expS=sb("expS", [128, 192], F32, ps_),
                    tmp64=sb("tmp64", [128, 64], F32, ps_), e64=sb("e64", [128, 64], F32, ps_), bias64=sb("bias64", [128, 64], F32, ps_),
                    w64=sb("w64", [128, 64], F32, ps_))

    def phase_ssd_bwd(l):
        with ExitStack() as ps_:
            wk = ssd_wk(ps_)
            xBs = [sb("xB%d" % i, [128, 3072], BF16, ps_) for i in range(2)]
            xw = sb("xw", [128, 2048], BF16, ps_)
            st = sb("st", [128, 2048], F32, ps_); st_bf = sb("st_bf", [128, 2048], BF16, ps_); tmpY = sb("tmpY", [128, 2048], F32, ps_)
            S.op('pool', lambda E: E.memset(st[:], 0.0), w=['st'])
            S.op('pool', lambda E: E.memset(st_bf[:], 0.0), w=['st_bf'])
            order = [1, 0] + list(range(17, 1, -1))
            for i, tt in enumerate(order):
                xB = xBs[i % 2]
                S.dma(q_sp, SBST[tt], st_bf[:], r=['st_bf'], w=['SBST'], semkey='oSB', acc=True)
                if i == len(order) - 1:
                    break
                S.dma(q_sp, xB[:], XBM[tt * 128:(tt + 1) * 128, :], w=['xB'], semkey='xB%d' % (i % 2))
                for _ in ssd_small(tt, wk):
                    pass
                for _ in ssd_state_update(xB, 'xB', xw, 32, 160, st, st_bf, tmpY, wk):
                    pass
            S.barrier()

    def merge_branch(yT, ykey, nk, Wo, wokey, br, gt, mT, tmpm, first, gkey='gt', mkey='mT', banks=(0, 1)):
        for fo in range(8):
            for k in range(nk):
                S.op('pe', lambda E, fo=fo, k=k: E.matmul(PS(banks[fo // 4], (fo % 4) * 128, 128), lhsT=Wo[:, k, fo * 128:(fo + 1) * 128], rhs=yT[:, k, :],
                                                          start=(k == 0), stop=(k == nk - 1)), r=[wokey, ykey], w=PK(banks[fo // 4]))
        for b in range(2):
            gv = gt[:, br * 8 + b * 4:br * 8 + b * 4 + 4, :].rearrange("p a t -> p (a t)")
            mv = mT[:, b * 4:b * 4 + 4, :].rearrange("p a t -> p (a t)")
            if first:
                S.op('dve', lambda E, b=b, gv=gv, mv=mv: E.tensor_tensor(out=mv, in0=PS(banks[b]), in1=gv, op=ALU.mult), r=PK(banks[b]) + [gkey], w=[mkey])
            else:
                tv = tmpm[:, b * 4:b * 4 + 4, :].rearrange("p a t -> p (a t)")
                S.op('dve', lambda E, b=b, gv=gv, tv=tv: E.tensor_tensor(out=tv, in0=PS(banks[b]), in1=gv, op=ALU.mult), r=PK(banks[b]) + [gkey], w=['tmpm'])
                S.op('pool', lambda E, mv=mv, tv=tv: E.tensor_tensor(out=mv, in0=mv, in1=tv, op=ALU.add), r=['tmpm', mkey], w=[mkey])

    def load_Wo(dst, src, key, nk):
        for k0 in range(0, nk, 4):
            S.dma(q_cast, dst[:, k0:k0 + 4, :], src[k0 * 128:(k0 + 4) * 128, :].rearrange("(k p) n -> p k n", p=128), w=[key], acc=True)

    MTS = dscr("MTS", [NT, 128, 8 * 128], F32)
    W1S = dscr("W1S", [8, 128, KC * 512], BF16)

    def phase_ssd_fwd(l):
        last = (l == DEPTH - 1)
        with ExitStack() as ps_:
            Wo = sb("Wo_ssd", [128, 16, D], BF16, ps_); sng = sb("sng", [128, 16], F32, ps_)
            sets = []
            for p in range(2):
                x = '_%d' % p
                sets.append(dict(sfx=x, wk=ssd_wk(ps_, x), xB=sb("xB" + x, [128, 3072], BF16, ps_), bct=sb("bct" + x, [128, 16, 128], BF16, ps_),
                                 zt=sb("zt" + x, [128, 2048], BF16, ps_), sbst=sb("sbst" + x, [128, 2048], BF16, ps_),
                                 gt=sb("gt" + x, [128, 8, 128], BF16, ps_), ahl=sb("ahl" + x, [128, 2, 64], BF16, ps_),
                                 CBm=[sb("CBm%d%s" % (i, x), [128, 8, 128], F32, ps_) for i in range(2)],
                                 xdt=[sb("xdt%d%s" % (i, x), [128, 2048], BF16, ps_) for i in range(2)]))
            st = sb("st", [128, 2048], F32, ps_); st_bf = sb("st_bf", [128, 2048], BF16, ps_)
            xw = sb("xw", [128, 2048], BF16, ps_)
            Lt = [[sb("Lt%d%d" % (d_, i), [128, 512], F32, ps_) for i in range(2)] for d_ in range(2)]
            Mt = [[sb("Mt%d%d" % (d_, i), [128, 512], BF16, ps_) for i in range(2)] for d_ in range(2)]
            Y = sb("Y", [128, 2048], F32, ps_); tmpY = sb("tmpY", [128, 2048], F32, ps_)
            yn = sb("yn", [128, 2048], BF16, ps_); yaT = sb("yaT", [128, 16, 128], BF16, ps_)
            ssq = sb("ssq", [128, 8], F32, ps_); junk = sb("junk2", [128, 256], BF16, ps_)
            mT = sb("mT", [128, 8, 128], F32, ps_); tmpY2 = sb("tmpY2", [128, 1024], F32, ps_)
            S.dma(q_sp, sng[:], sng_in[l], w=['sng'])
            load_Wo(Wo, w_o_ssd[l], 'Wo', 16)

            def scale_Wo():
                for k in range(16):
                    S.op('act', lambda E, k=k: E.activation(out=Wo[:, k, :], in_=Wo[:, k, :], func=AF.Copy, scale=sng[:, k:k + 1]),
                         r=['Wo', 'sng'], w=['Wo'])
            S.op('pool', lambda E: E.memset(st[:], 0.0), w=['st'])
            S.op('pool', lambda E: E.memset(st_bf[:], 0.0), w=['st_bf'])

            def need_y(tt):
                return not (last and tt < 2)

            def stage_A(tt):
                s_ = sets[tt % 2]; x = s_['sfx']; wk = s_['wk']
                xB, bct, zt, sbst, gt, ahl, CBm, xdt = (s_[k] for k in ('xB', 'bct', 'zt', 'sbst', 'gt', 'ahl', 'CBm', 'xdt'))
                S.dma(q_sp, xB[:], XBM[tt * 128:(tt + 1) * 128, :], w=['xB' + x])
                for _ in ssd_small(tt, wk):
                    pass
                if not need_y(tt):
                    return
                S.dma(q_sp, bct[:], BCT[:, tt * 128:(tt + 1) * 128].rearrange("(c p) t -> p c t", p=128), w=['bct' + x])
                S.dma(q_sp, zt[:], ZTM[tt * 128:(tt + 1) * 128, :], w=['zt' + x])
                S.dma(q_sp, sbst[:], SBST[tt], w=['sbst' + x])
                S.dma(q_sp, gt[:], GFM[0:1024, tt * 128:(tt + 1) * 128].rearrange("(c p) t -> p c t", p=128), w=['gt' + x])
                a2 = wk['a2']
                S.op('act', lambda E: E.activation(out=ahl[:, 0, :], in_=a2[:], func=AF.Copy), r=['a2' + x], w=['ahl' + x])
                S.op('dve', lambda E: E.tensor_tensor(out=ahl[:, 1, :], in0=a2[:], in1=ahl[:, 0, :], op=ALU.subtract), r=['a2' + x, 'ahl' + x], w=['ahl' + x])
                for g in range(8):
                    S.op('pe', lambda E, g=g: E.matmul(PS(6 + g // 4, (g % 4) * 128, 128), lhsT=bct[:, g, :], rhs=bct[:, 8 + g, :], start=True, stop=True),
                         r=['bct' + x], w=PK(6 + g // 4))
                for dr, cm in enumerate((C_MF01, C_MB01)):
                    S.op('dve', lambda E, dr=dr, cm=cm: E.tensor_tensor(out=CBm[dr][:], in0=psall[:, 6 * 512:8 * 512].rearrange("p (g t) -> p g t", t=128),
                                                                      in1=cst[:, cm, :].unsqueeze(1).broadcast_to([128, 8, 128]), op=ALU.mult),
                         r=PK(6, 7) + ['cst'], w=['CBm%d%s' % (dr, x)])
                for dr in range(2):
                    S.op('pool', lambda E, dr=dr: E.tensor_tensor(out=v3(xdt[dr][:], 64), in0=v3(xB[:, 0:2048], 64),
                                                                 in1=bc3(dt_all[:, tt, dr * 32:dr * 32 + 32], 64), op=ALU.mult),
                         r=['xB' + x, 'dt_all'], w=['xdt%d%s' % (dr, x)])

            def stage_B(tt):
                if not need_y(tt):
                    return
                s_ = sets[tt % 2]; x = s_['sfx']
                ahl, CBm, xdt = s_['ahl'], s_['CBm'], s_['xdt']

                def emit_E(g):
                    sl = g % 2
                    for dr in range(2):
                        c1 = C_MLE if dr == 0 else C_NMLT
                        c2 = C_NMLE if dr == 0 else C_LW
                        cmk = C_NMF if dr == 0 else C_NMB
                        for hh in range(4):
                            hd = dr * 32 + g * 4 + hh
                            Er = PS(dr, hh * 128, 128)
                            for v in range(2):
                                S.op('pe', lambda E, Er=Er, hd=hd, v=v, c1=c1: E.matmul(Er, lhsT=ahl[:, v, hd:hd + 1].broadcast_to([128, 128]), rhs=cstb[:, c1, :],
                                                                                      start=(v == 0), stop=False), r=['ahl' + x, 'cstb'], w=PK(dr))
                                S.op('pe', lambda E, Er=Er, hd=hd, v=v, c2=c2: E.matmul(Er, lhsT=cstb[:, c2, :], rhs=ahl[:, v, hd:hd + 1].broadcast_to([128, 128]),
                                                                                      start=False, stop=False), r=['ahl' + x, 'cstb'], w=PK(dr))
                            S.op('pe', lambda E, Er=Er, cmk=cmk: E.matmul(Er, lhsT=identb, rhs=cstb[:, cmk, :], start=False, stop=True), r=['cstb'], w=PK(dr))
                        S.op('act', lambda E, dr=dr, sl=sl: E.activation(out=Lt[dr][sl][:], in_=PS(dr), func=AF.Exp), r=PK(dr), w=['Lt%d%d' % (dr, sl)])
                        S.op('dve' if (dr == 0 or g % 2 == 0) else 'pool', lambda E, dr=dr, sl=sl, g=g: E.tensor_tensor(out=Mt[dr][sl][:].rearrange("p (h t) -> p h t", t=128),
                                                                               in0=Lt[dr][sl][:].rearrange("p (h t) -> p h t", t=128),
                                                                               in1=CBm[dr][:, g, :].unsqueeze(1).broadcast_to([128, 4, 128]), op=ALU.mult),
                             r=['Lt%d%d' % (dr, sl), 'CBm%d%s' % (dr, x)], w=['Mt%d%d' % (dr, sl)])

                def emit_Y(g):
                    sl = g % 2
                    for hh in range(4):
                        h = g * 4 + hh
                        for dr in range(2):
                            S.op('pe', lambda E, sl=sl, dr=dr, h=h, hh=hh: E.matmul(PS(2 + h // 8, (h % 8) * 64, 64), lhsT=Mt[dr][sl][:, hh * 128:(hh + 1) * 128],
                                                                                 rhs=xdt[dr][:, h * 64:(h + 1) * 64], start=(dr == 0), stop=(dr == 1)),
                                 r=['Mt%d%d' % (dr, sl), 'xdt%d%s' % (dr, x)], w=PK(2 + h // 8))

                emit_E(0)
                yield
                for g in range(1, 8):
                    emit_E(g)
                    emit_Y(g - 1)
                    yield
                emit_Y(7)
                yield

            def stage_C0(tt):
                if not need_y(tt):
                    return
                s_ = sets[tt % 2]; x = s_['sfx']; wk = s_['wk']
                bct, sbst = s_['bct'], s_['sbst']
                S.op('dve', lambda E: E.tensor_copy(out=Y[:], in_=psall[:, 2 * 512:6 * 512]), r=PK(2, 3, 4, 5), w=['Y'])
                for dr, (stb, skey) in enumerate(((st_bf, 'st_bf'), (sbst, 'sbst' + x))):
                    for half in range(2):
                        yield
                        for gg in range(4):
                            g = half * 4 + gg
                            S.op('pe', lambda E, g=g, gg=gg, stb=stb: E.matmul(PS(6 + gg // 2, (gg % 2) * 256, 256), lhsT=bct[:, 8 + g, :],
                                                                               rhs=stb[:, g * 256:(g + 1) * 256], start=True, stop=True),
                                 r=['bct' + x, skey], w=PK(6 + gg // 2))
                        c0 = half * 1024
                        S.op('dve', lambda E, c0=c0, dr=dr, half=half: E.tensor_tensor(
                            out=v3(tmpY[:, c0:c0 + 1024], 64), in0=v3(psall[:, 6 * 512:8 * 512], 64),
                            in1=bc3(wk['e64'][:, dr * 32 + half * 16:dr * 32 + half * 16 + 16], 64), op=ALU.mult),
                            r=PK(6, 7) + ['e64' + x], w=['tmpY'])
                        S.op('pool', lambda E, c0=c0: E.tensor_tensor(out=Y[:, c0:c0 + 1024], in0=Y[:, c0:c0 + 1024], in1=tmpY[:, c0:c0 + 1024], op=ALU.add),
                             r=['Y', 'tmpY'], w=['Y'])

            def stage_C1(tt):
                s_ = sets[tt % 2]; x = s_['sfx']; wk = s_['wk']
                xB, zt, gt = s_['xB'], s_['zt'], s_['gt']
                if need_y(tt):
                    yield
                    S.op('pool', lambda E: E.tensor_tensor(out=v3(tmpY[:], 64), in0=v3(xB[:, 0:2048], 64), in1=bc3(Dbc[:, 0:32], 64), op=ALU.mult),
                         r=['xB' + x, 'Dbc'], w=['tmpY'])
                    S.op('pool', lambda E: E.tensor_tensor(out=Y[:], in0=Y[:], in1=tmpY[:], op=ALU.add), r=['Y', 'tmpY'], w=['Y'])
                    S.op('pool', lambda E: E.tensor_tensor(out=Y[:], in0=Y[:], in1=zt[:], op=ALU.mult), r=['Y', 'zt' + x], w=['Y'])
                    yield
                    for g in range(8):
                        S.op('act', lambda E, g=g: E.activation(out=junk[:], in_=Y[:, g * 256:(g + 1) * 256], func=AF.Square, accum_out=ssq[:, g:g + 1]),
                             r=['Y'], w=['junk2', 'ssq'])
                    S.op('dve', lambda E: E.tensor_scalar(out=ssq[:], in0=ssq[:], scalar1=1.0 / 256, scalar2=EPS, op0=ALU.mult, op1=ALU.add), r=['ssq'], w=['ssq'])
                    S.op('act', lambda E: E.activation(out=ssq[:], in_=ssq[:], func=AF.Sqrt), r=['ssq'], w=['ssq'])
                    S.op('dve', lambda E: E.reciprocal(out=ssq[:], in_=ssq[:]), r=['ssq'], w=['ssq'])
                    S.op('dve', lambda E: E.tensor_tensor(out=v3(yn[:], 256), in0=v3(Y[:], 256), in1=bc3(ssq[:, 0:8], 256), op=ALU.mult), r=['Y', 'ssq'], w=['yn'])
                    yield
                    for k in range(16):
                        S.op('pe', lambda E, k=k: E.transpose(out=PSB(6 + k // 8)[:, (k % 8) * 128:(k % 8 + 1) * 128], in_=yn[:, k * 128:(k + 1) * 128], identity=identb),
                             r=['yn', 'cstb'], w=PK(6 + k // 8))
                    for b in range(2):
                        S.op('act', lambda E, b=b: E.activation(out=yaT[:, b * 8:(b + 1) * 8, :].rearrange("p a t -> p (a t)"), in_=PSB(6 + b), func=AF.Copy),
                             r=PK(6 + b), w=['yaT'])
                    yield
                    for fo in range(8):
                        if fo == 4:
                            yield
                        for k in range(16):
                            S.op('pe', lambda E, fo=fo, k=k: E.matmul(PS(6 + fo // 4, (fo % 4) * 128, 128), lhsT=Wo[:, k, fo * 128:(fo + 1) * 128], rhs=yaT[:, k, :],
                                                                      start=(k == 0), stop=(k == 15)), r=['Wo', 'yaT'], w=PK(6 + fo // 4))
                    for b in range(2):
                        gv = gt[:, b * 4:b * 4 + 4, :].rearrange("p a t -> p (a t)")
                        mv = mT[:, b * 4:b * 4 + 4, :].rearrange("p a t -> p (a t)")
                        S.op('dve', lambda E, b=b, gv=gv, mv=mv: E.tensor_tensor(out=mv, in0=PS(6 + b), in1=gv, op=ALU.mult), r=PK(6 + b) + ['gt' + x], w=['mT'])
                    S.dma(q_sp, MTS[tt], mT[:].rearrange("p a t -> p (a t)"), r=['mT'], w=['MTS'], semkey='oMT', acc=True)

            def stage_SU(tt):
                s_ = sets[tt % 2]
                if tt != NT - 1:
                    for _ in ssd_state_update(s_['xB'], 'xB' + s_['sfx'], xw, 0, 128, st, st_bf, tmpY2, s_['wk']):
                        pass

            def stage_C(tt):
                yield from stage_C0(tt)
                yield
                stage_SU(tt)
                yield from stage_C1(tt)

            def interleave(*gens):
                gens = [g for g in gens if g is not None]
                while gens:
                    for g in list(gens):
                        try:
                            next(g)
                        except StopIteration:
                            gens.remove(g)

            stage_A(0)
            if need_y(0):
                interleave(stage_B(0))
                scale_Wo()
            else:
                scale_Wo()
            for tt in range(NT):
                if tt + 1 < NT:
                    stage_A(tt + 1)
                    interleave(stage_C(tt), stage_B(tt + 1))
                else:
                    interleave(stage_C(tt))
            S.barrier()

    def phase_attn(l, kind):
        last = (l == DEPTH - 1)
        na = (kind == 'na')
        nkv = 16 if na else 4
        R = 6 if na else 4
        br = 1 if na else 2
        with ExitStack() as ps_:
            Wo = sb("Wo_att", [128, 8, D], BF16, ps_)
            load_Wo(Wo, (w_o_na if na else w_o_swa)[l], 'Wo', 8)
            if na:
                nabt = sb("nabt", [128, 16, NB_NA, 128], BF16, ps_)
                nst = [sb("nst%d" % i, [128, NB_NA, 128], F32, ps_) for i in range(2)]

                def load_nabt():
                    for h in range(16):
                        st_ = nst[h % 2]; nk = 'nst%d' % (h % 2)
                        S.dma(q_sp, st_[:], nab_in[l, h].rearrange("b k q -> k b q"), w=[nk])
                        S.op('act', lambda E, h=h, st_=st_: E.activation(out=nabt[:, h, :, :], in_=st_[:], func=AF.Copy), r=[nk], w=['nabt'])
            else:
                wout = sb("wout", [128, 8, D], BF16, ps_)
                load_Wo(wout, w_out[l], 'wout', 8)
                xts = [sb("xt%d" % i, [128, D], F32, ps_) for i in range(2)]
                tmpx = sb("tmpx", [128, D], F32, ps_); mTb = sb("mTb", [128, 8, 128], BF16, ps_)
            kts = [sb("kt%d" % i, [128, 8 if na else 4, 128], BF16, ps_) for i in range(R + 2)]
            vts = [sb("vt%d" % i, [128, nkv, 64], BF16, ps_) for i in range(R + 2)]
            qTs = [sb("qT%d" % i, [128, 8, 2, 128], BF16, ps_) for i in range(2)]
            gts = [sb("gt%d" % i, [128, 8, 128], BF16, ps_) for i in range(2)]
            mTs = [sb("mT%d" % i, [128, 8, 128], F32, ps_) for i in range(2)]
            for i in range(2):
                S.op('pool', lambda E, i=i: E.memset(qTs[i][:], 0.0), w=['qTz%d' % i])
            nkmax = (5 if na else 3) + 2
            Eb = [sb("Eb%d" % i, [128, nkmax, 1024], BF16, ps_) for i in range(2)]
            den = sb("den", [128, 16], F32, ps_); yatt = sb("yatt", [128, D], BF16, ps_); yT = sb("yT", [128, 8, 128], BF16, ps_)
            tmpm = sb("tmpm", [128, 8, 128], F32, ps_)
            KS = NAK if na else SWK; VS = NAV if na else SWV; QS = NAQ if na else SWQ

            def load_kv(i, ktt):
                kk = 'kt%d' % i; vk = 'vt%d' % i
                if na:
                    S.dma(q_sp, kts[i][:], KS[:, ktt * 128:(ktt + 1) * 128].rearrange("(k p) t -> p k t", p=128), w=[kk])
                else:
                    for hf in range(2):
                        S.dma(q_sp, kts[i][hf * 64:(hf + 1) * 64, :, :], KS[:, ktt * 128:(ktt + 1) * 128].rearrange("(g d) t -> d g t", d=64), w=[kk], acc=(hf == 1))
                S.dma(q_sp, vts[i][:].rearrange("p h d -> p (h d)"), VS[ktt * 128:(ktt + 1) * 128, :], w=[vk])

            def local_tiles(t):
                if na:
                    return na_key_tiles(t)
                return [u for u in (t - 1, t, t + 1) if 0 <= u <= 15]

            load_kv(R, 0); load_kv(R + 1, 1)
            ids = na_ids()
            tiles = [tt for tt in range(NT) if not (last and tt < 2)]
            ring = {'upto': -1}

            def issue_loads(idx):
                tt = tiles[idx]; p = idx % 2
                qsrc = QS[:, tt * 128:(tt + 1) * 128].rearrange("(k p) t -> p k t", p=128)
                S.dma(q_sp, qTs[p][0:64, :, 0, :], qsrc[0:64], r=['qTz%d' % p], w=['qTa%d' % p])
                S.dma(q_sp, qTs[p][64:128, :, 1, :], qsrc[64:128], r=['qTz%d' % p], w=['qTb%d' % p])
                S.dma(q_sp, gts[p][:], GFM[br * 1024:(br + 1) * 1024, tt * 128:(tt + 1) * 128].rearrange("(c p) t -> p c t", p=128), w=['gt%d' % p])
                S.dma(q_sp, mTs[p][:].rearrange("p a t -> p (a t)"), MTS[tt], r=['MTS'], w=['mT%d' % p])
                if not na:
                    S.dma(q_sp, xts[p][:], xres[tt * 128:(tt + 1) * 128, :], r=['xres'], w=['xt%d' % p])
                if tt >= 2:
                    for u in local_tiles(tt - 2):
                        if u > ring['upto']:
                            load_kv(u % R, u + 2)
                            ring['upto'] = u

            issue_loads(0)
            if na:
                load_nabt()
            scst = {'sc': 0}
            for idx, tt in enumerate(tiles):
                if idx + 1 < len(tiles):
                    issue_loads(idx + 1)
                p = idx % 2
                qT = qTs[p]; gt = gts[p]; mT = mTs[p]
                qk = ['qTa%d' % p, 'qTb%d' % p, 'qTz%d' % p]

                def make_keys(tt):
                    keys = []
                    if tt >= 2:
                        t = tt - 2
                        if na:
                            for u in local_tiles(t):
                                bid = ids[(t, u)]
                                keys.append((u % R, (lambda h, bid=bid: nabt[:, h, bid, :]), 'nabt'))
                        else:
                            for u in local_tiles(t):
                                cm = C_SWLO if u == t - 1 else (C_SWHI if u == t + 1 else None)
                                keys.append((u % R, (None if cm is None else (lambda h, cm=cm: cstb[:, cm, :])), 'cstb'))
                    keys.append((R, None, None)); keys.append((R + 1, None, None))
                    return keys

                keys = make_keys(tt)
                nk_ = len(keys)
                def emit_scores(hb, keys, qT, qk, scst):
                    eb = Eb[hb % 2]; ek = 'Eb%d' % (hb % 2)
                    for ui, (bi, bfn, bkey) in enumerate(keys):
                        base = 0 if scst['sc'] % 2 == 0 else 6
                        scst['sc'] += 1
                        for i in range(8):
                            h = hb * 8 + i
                            bank = base + (i % 2); off = (i // 2) * 128
                            kt = kts[bi][:, h // 2, :] if na else kts[bi][:, h // 4, :]
                            S.op('pe', lambda E, bank=bank, off=off, kt=kt, h=h, bfn=bfn: E.matmul(PS(bank, off, 128), lhsT=kt, rhs=qT[:, h // 2, h % 2, :],
                                                                                               start=True, stop=(bfn is None)),
                                 r=['kt%d' % bi] + qk, w=PK(bank))
                            if bfn is not None:
                                S.op('pe', lambda E, bank=bank, off=off, h=h, bfn=bfn: E.matmul(PS(bank, off, 128), lhsT=identb, rhs=bfn(h), start=False, stop=True),
                                     r=['cstb', bkey], w=PK(bank))
                        S.op('act', lambda E, base=base, ui=ui, eb=eb: E.activation(out=eb[:, ui, :], in_=psall[:, base * 512:base * 512 + 1024], func=AF.Exp),
                             r=PK(base, base + 1), w=[ek])

                if idx == 0:
                    emit_scores(0, keys, qT, qk, scst)
                emit_scores(1, keys, qT, qk, scst)
                for hb in range(2):
                    eb = Eb[hb % 2]; ek = 'Eb%d' % (hb % 2)
                    for i in range(8):
                        h = hb * 8 + i
                        eoff = (i % 2) * 512 + (i // 2) * 128
                        hv = h if na else h // 4
                        ob = 2 + h // 8
                        for ui, (bi, bfn, bkey) in enumerate(keys):
                            S.op('pe', lambda E, ob=ob, h=h, hv=hv, ui=ui, bi=bi, eoff=eoff, eb=eb: E.matmul(PS(ob, (h % 8) * 64, 64), lhsT=eb[:, ui, eoff:eoff + 128],
                                                                                                         rhs=vts[bi][:, hv, :], start=(ui == 0), stop=(ui == nk_ - 1)),
                                 r=[ek, 'vt%d' % bi], w=PK(ob))
                            S.op('pe', lambda E, h=h, ui=ui, eoff=eoff, eb=eb: E.matmul(PS(4, h * 2, 2), lhsT=eb[:, ui, eoff:eoff + 128],
                                                                                       rhs=cstb[:, C_ONES, 0:2], start=(ui == 0), stop=(ui == nk_ - 1)),
                                 r=[ek, 'cstb'], w=PK(4))
                S.op('dve', lambda E: E.tensor_copy(out=den[:], in_=PS(4, 0, 32).rearrange("p (h two) -> p h two", two=2)[:, :, 0]), r=PK(4), w=['den'])
                if not na:
                    S.op('dve', lambda E: E.tensor_tensor(out=den[:], in0=den[:], in1=sinkexp[:], op=ALU.add), r=['den', 'sinkexp'], w=['den'])
                S.op('dve', lambda E: E.reciprocal(out=den[:], in_=den[:]), r=['den'], w=['den'])
                for b in range(2):
                    S.op('dve', lambda E, b=b: E.tensor_tensor(out=v3(yatt[:, b * 512:(b + 1) * 512], 64), in0=v3(PS(2 + b), 64),
                                                               in1=bc3(den[:, b * 8:b * 8 + 8], 64), op=ALU.mult), r=PK(2 + b) + ['den'], w=['yatt'])
                if idx + 1 < len(tiles):
                    pn = (idx + 1) % 2
                    emit_scores(0, make_keys(tiles[idx + 1]), qTs[pn], ['qTa%d' % pn, 'qTb%d' % pn, 'qTz%d' % pn], scst)
                for k in range(8):
                    S.op('pe', lambda E, k=k: E.transpose(out=PSB(5)[:, k * 128:(k + 1) * 128], in_=yatt[:, k * 128:(k + 1) * 128], identity=identb),
                         r=['yatt', 'cstb'], w=PK(5))
                S.op('act', lambda E: E.activation(out=yT[:].rearrange("p a t -> p (a t)"), in_=PSB(5), func=AF.Copy), r=PK(5), w=['yT'])
                merge_branch(yT, 'yT', 8, Wo, 'Wo', 0, gt, mT, tmpm, False, gkey='gt%d' % p, mkey='mT%d' % p, banks=(4, 5))
                if na:
                    S.dma(q_sp, MTS[tt], mT[:].rearrange("p a t -> p (a t)"), r=['mT%d' % p], w=['MTS'], semkey='oMT%d' % p, acc=True)
                else:
                    tk = tok_type(tt)
                    xt = xts[p]; xk = 'xt%d' % p
                    S.op('act', lambda E, mT=mT: E.activation(out=mTb[:], in_=mT[:], func=AF.Copy), r=['mT%d' % p], w=['mTb'])
                    for half in range(2):
                        for k in range(8):
                            S.op('pe', lambda E, half=half, k=k: E.matmul(PS(4 + half), lhsT=mTb[:, k, :], rhs=wout[:, k, half * 512:(half + 1) * 512],
                                                                        start=(k == 0), stop=(k == 7)), r=['mTb', 'wout'], w=PK(4 + half))
                        hs = slice(half * 512, (half + 1) * 512)
                        S.op('dve', lambda E, half=half, hs=hs, tk=tk: E.tensor_tensor(out=tmpx[:, hs], in0=PS(4 + half), in1=gbc[:, tk, 0, hs], op=ALU.mult),
                             r=PK(4 + half) + ['gbc'], w=['tmpx'])
                        S.op('pool', lambda E, hs=hs, xt=xt: E.tensor_tensor(out=xt[:, hs], in0=xt[:, hs], in1=tmpx[:, hs], op=ALU.add), r=['tmpx', xk], w=[xk])
                    S.dma(q_sp, xres[tt * 128:(tt + 1) * 128, :], xt[:], r=[xk], w=['xres'], semkey='oxres%d' % p, acc=True)
            S.barrier()

    def phase_ffn(l):
        last = (l == DEPTH - 1)
        with ExitStack() as ps_:
            w2 = sb("w2", [128, 32, D], BF16, ps_)
            w1 = [sb("w1_%d" % i, [128, KC, 512], BF16, ps_) for i in range(2)]
            h2Ts = [sb("h2T%d" % i, [128, KC, 512], BF16, ps_) for i in range(2)]
            uT = sb("uT", [128, 32, 512], BF16, ps_)
            xns = [sb("xfn%d" % i, [128, D], F32, ps_) for i in range(2)]
            xus = [sb("xfu%d" % i, [128, D], F32, ps_) for i in range(2)]
            rl = [sb("rl%d" % i, [128, 512], F32, ps_) for i in range(2)]
            tmpx = sb("tmpx", [128, D], F32, ps_)
            wk = dict(junk=sb("junk", [128, D], BF16, ps_), ss=sb("ss", [128, 1], F32, ps_), xn=sb("xn", [128, D], BF16, ps_))
            blocks = [b_ for b_ in (1, 2, 3, 4, 0) if not (last and b_ == 0)]
            st_ = {'wi': 0, 'ri': 0, 'ni': 0, 'ui': 0, 'w2': False}

            def do_norm(bi):
                blk = blocks[bi]; t0, n = BLKS[blk]
                h2T = h2Ts[bi % 2]; hk = 'h2T%d' % (bi % 2)
                for i in range(n // 128):
                    tt = t0 // 128 + i
                    j = st_['ni'] % 2; st_['ni'] += 1
                    S.dma(q_sp, xns[j][:], xres[tt * 128:(tt + 1) * 128, :], r=['xres'], w=['xfn%d' % j])
                    norm_to_T(l, tt, xns[j][:], 'xfn%d' % j, G2, S2, h2T, hk, i * 128, wk)

            def do_ffn1(bi):
                blk = blocks[bi]; t0, n = BLKS[blk]
                h2T = h2Ts[bi % 2]; hk = 'h2T%d' % (bi % 2)
                for g in range(8):
                    w = w1[st_['wi'] % 2]; wkey = 'w1_%d' % (st_['wi'] % 2); st_['wi'] += 1
                    if bi == 0:
                        cast_load(w[:], w_ff1[l, :, g * 512:(g + 1) * 512].rearrange("(k p) n -> p k n", p=128), wkey)
                        S.dma(q_sp, W1S[g], w[:].rearrange("p k n -> p (k n)"), r=[wkey], w=['W1S'], semkey='o' + wkey, acc=True)
                    else:
                        S.dma(q_sp, w[:].rearrange("p k n -> p (k n)"), W1S[g], r=['W1S'], w=[wkey])
                    if g == 1 and not st_['w2']:
                        st_['w2'] = True
                        load_Wo(w2, w_ff2[l], 'w2', 32)
                    for j in range(4):
                        c = g * 4 + j
                        bank = c % 4
                        for k in range(KC):
                            S.op('pe', lambda E, k=k, j=j, w=w, bank=bank, n=n: E.matmul(PS(bank, 0, n), lhsT=w[:, k, j * 128:(j + 1) * 128], rhs=h2T[:, k, 0:n],
                                                                                       start=(k == 0), stop=(k == KC - 1)), r=[wkey, hk], w=PK(bank))
                        r_ = rl[st_['ri'] % 2]; rk = 'rl%d' % (st_['ri'] % 2); st_['ri'] += 1
                        S.op('act', lambda E, r_=r_, bank=bank, n=n: E.activation(out=r_[:, 0:n], in_=PS(bank, 0, n), func=AF.Relu), r=PK(bank), w=[rk])
                        S.op('pool', lambda E, r_=r_, c=c, n=n: E.tensor_tensor(out=uT[:, c, 0:n], in0=r_[:, 0:n], in1=r_[:, 0:n], op=ALU.mult), r=[rk], w=['uT'])

            def do_ffn2(bi):
                blk = blocks[bi]; t0, n = BLKS[blk]
                for i in range(n // 128):
                    tt = t0 // 128 + i
                    tk = tok_type(tt)
                    j = st_['ui'] % 2; st_['ui'] += 1
                    xu = xus[j]; xk = 'xfu%d' % j
                    S.dma(q_sp, xu[:], xres[tt * 128:(tt + 1) * 128, :], r=['xres'], w=[xk])
                    for half in range(2):
                        for c in range(32):
                            S.op('pe', lambda E, half=half, c=c, i=i: E.matmul(PS(6 + half), lhsT=uT[:, c, i * 128:(i + 1) * 128], rhs=w2[:, c, half * 512:(half + 1) * 512],
                                                                             start=(c == 0), stop=(c == 31)), r=['uT', 'w2'], w=PK(6 + half))
                        hs = slice(half * 512, (half + 1) * 512)
                        S.op('dve', lambda E, half=half, hs=hs, tk=tk: E.tensor_tensor(out=tmpx[:, hs], in0=PS(6 + half), in1=gbc[:, tk, 1, hs], op=ALU.mult),
                             r=PK(6 + half) + ['gbc'], w=['tmpx'])
                        S.op('pool', lambda E, hs=hs, xu=xu: E.tensor_tensor(out=xu[:, hs], in0=xu[:, hs], in1=tmpx[:, hs], op=ALU.add), r=['tmpx', xk], w=[xk])
                    S.dma(q_sp, xres[tt * 128:(tt + 1) * 128, :], xu[:], r=[xk], w=['xres'], semkey='o' + xk, acc=True)

            do_norm(0)
            for bi in range(len(blocks)):
                do_ffn1(bi)
                if bi + 1 < len(blocks):
                    do_norm(bi + 1)
                do_ffn2(bi)
            S.barrier()

    def phase_final():
        with ExitStack() as ps_:
            xts = [sb("xo%d" % i, [128, D], F32, ps_) for i in range(2)]
            junk = sb("junk", [128, D], BF16, ps_); ss = sb("ss", [128, 1], F32, ps_)
            for t in range(16):
                xt = xts[t % 2]; xk = 'xo%d' % (t % 2)
                S.dma(q_sp, xt[:], xres[(t + 2) * 128:(t + 3) * 128, :], r=['xres'], w=[xk])
                S.op('act', lambda E, xt=xt: E.activation(out=junk[:], in_=xt[:], func=AF.Square, accum_out=ss[:]), r=[xk], w=['junk', 'ss'])
                S.op('dve', lambda E: E.tensor_scalar(out=ss[:], in0=ss[:], scalar1=1.0 / D, scalar2=EPS, op0=ALU.mult, op1=ALU.add), r=['ss'], w=['ss'])
                S.op('act', lambda E: E.activation(out=ss[:], in_=ss[:], func=AF.Sqrt), r=['ss'], w=['ss'])
                S.op('dve', lambda E: E.reciprocal(out=ss[:], in_=ss[:]), r=['ss'], w=['ss'])
                S.op('dve', lambda E, xt=xt: E.scalar_tensor_tensor(out=xt[:], in0=xt[:], scalar=ss[:, 0:1], in1=fing[:], op0=ALU.mult, op1=ALU.mult),
                     r=[xk, 'ss', 'fing'], w=[xk])
                S.dma(q_sp, out_d[t * 128:(t + 1) * 128, :], xt[:], r=[xk], w=['out'], semkey='o' + xk, acc=True)
            S.barrier()

    for l in range(n_layers):
        phase_adaln(l)
        if stop_after == 'adaln':
            break
        phase_proj(l)
        if stop_after == 'proj':
            break
        phase_ssd_fwd(l)
        if stop_after == 'ssdf':
            break
        phase_attn(l, 'na')
        if stop_after == 'na':
            break
        phase_attn(l, 'swa')
        if stop_after == 'swa':
            break
        phase_ffn(l)
    if stop_after is None and n_layers == DEPTH:
        phase_final()

    if dbg:
        dmod = nc.dram_tensor("dbg_modT", [128, 96], F32, kind="ExternalOutput").ap()
        S.dma(q_sp, dmod[:, :], modT[:].rearrange("p j t -> p (j t)"), r=['modT'], w=['dbg_modT'])
        dgbc = nc.dram_tensor("dbg_gbc", [128, 4 * D], F32, kind="ExternalOutput").ap()
        S.dma(q_sp, dgbc[:, :], gbc[:].rearrange("p a b d -> p (a b d)"), r=['gbc'], w=['dbg_gbc'])
        ddt = nc.dram_tensor("dbg_dt", [128, NT * 64], F32, kind="ExternalOutput").ap()
        S.dma(q_sp, ddt[:, :], dt_all[:].rearrange("p a b -> p (a b)"), r=['dt_all'], w=['dbg_dt'])
    S.finish()
    es.close()
    return nc


def prep_inputs(inputs):
    f = lambda a: np.ascontiguousarray(np.asarray(a, dtype=np.float32))
    x = f(inputs['x']); c = f(inputs['c']); ctx = f(inputs['ctx']); c_ctx = f(inputs['c_ctx'])
    shared = {}
    shared['ada_w'] = f(inputs['ada_w'])
    shared['ada_b2'] = f(np.stack([inputs['ada_b'], inputs['ada_b']], 1))
    shared['n1g'] = f(np.asarray(inputs['norm1_g']).reshape(DEPTH, KC, 128).transpose(0, 2, 1))
    shared['n2g'] = f(np.asarray(inputs['norm2_g']).reshape(DEPTH, KC, 128).transpose(0, 2, 1))
    shared['final_g'] = f(np.asarray(inputs['final_g']).reshape(1, D))
    shared['w_in'] = f(inputs['w_in'])
    shared['cw'] = f(np.asarray(inputs['conv_w']).reshape(DEPTH, 5, 32, 128).transpose(0, 3, 2, 1))
    shared['cb'] = f(np.asarray(inputs['conv_b']).reshape(DEPTH, 32, 128).transpose(0, 2, 1))
    shared['dtb'] = f(np.asarray(inputs['dt_bias']).reshape(DEPTH, 1, 64))
    shared['alog'] = f(np.asarray(inputs['a_log']).reshape(DEPTH, 1, 64))
    shared['ssdd'] = f(np.asarray(inputs['ssd_d']).reshape(DEPTH, 1, 32))
    shared['sng'] = f(np.asarray(inputs['ssd_norm_g']).reshape(DEPTH, 16, 128).transpose(0, 2, 1))
    rpb = np.asarray(inputs['na_rpb'], np.float32)
    shared['nab'] = f(np.stack([build_na_tables(rpb[l])[0] for l in range(DEPTH)], 0))
    shared['sink'] = f(np.asarray(inputs['swa_sink']).reshape(DEPTH, 1, 16))
    for k in ('w_o_ssd', 'w_o_na', 'w_o_swa', 'w_out', 'w_ff1', 'w_ff2'):
        shared[k] = f(inputs[k])
    shared['cst'] = host_consts()
    rc, rs = host_rope()
    shared['ropec'] = rc; shared['ropes'] = rs
    maps = []
    for b in range(x.shape[0]):
        m = dict(shared)
        m['x'] = x[b]; m['ctx'] = ctx[b]
        cc = np.stack([c[b].reshape(KC, 128).T, c_ctx.reshape(KC, 128).T], -1)
        m['cc'] = f(cc)
        maps.append(m)
    return maps


def kernel(**inputs):
    maps = prep_inputs(inputs)
    nc = bass.Bass("TRN2", target_bir_lowering=False)
    build(nc)
    res = run_bass_kernel_spmd(nc, maps, core_ids=list(range(8)))
    return np.stack([r['out'] for r in res.results], 0).astype(np.float32)
```

```python
import numpy as np
import concourse.bass as bass
import concourse.mybir as mybir
from concourse.bass_utils import run_bass_kernel_spmd
from contextlib import ExitStack
import types


def _freeze(fn):
    if fn.__closure__:
        cells = []
        for c in fn.__closure__:
            try:
                cells.append(types.CellType(c.cell_contents))
            except ValueError:
                cells.append(c)
        return types.FunctionType(fn.__code__, fn.__globals__, fn.__name__, fn.__defaults__, tuple(cells))
    return fn

F32 = mybir.dt.float32
BF16 = mybir.dt.bfloat16
AF = mybir.ActivationFunctionType
ALU = mybir.AluOpType
AX = mybir.AxisListType

D = 1024; L = 2048; NCTX = 256; T = 2304; NT = 18; KC = 8; DEPTH = 4
IN_COLS = 13888
XBC_OFF = 0; DT_OFF = 4096; NA_K_OFF = 4160; NA_V_OFF = 5184; SWA_K_OFF = 6208; SWA_V_OFF = 6464
Z_OFF = 6720; NA_Q_OFF = 8768; SWA_Q_OFF = 9792; GATE_OFF = 10816
EPS = 1e-6
NEG = -30000.0
C_ID, C_MLE, C_NMLT, C_U, C_LW, C_ONES, C_NMF, C_NMB, C_MF01, C_MB01, C_SWLO, C_SWHI, C_NMLE = range(13)
NCST = 13
BLKS = [(0, 256)] + [(256 + 512 * i, 512) for i in range(4)]


class Sched:
    CE = ('pe', 'act', 'dve', 'pool')
    ALLQ = ('pe', 'act', 'dve', 'pool', 'sp')

    def __init__(self, nc, es):
        self.nc = nc
        self.es = es
        self.ops = {e: [] for e in self.ALLQ}
        self.sem = {e: es.enter_context(nc.semaphore('c_' + e)) for e in self.CE}
        self.cnt = {e: 0 for e in self.CE}
        self.seen = {e: {} for e in self.ALLQ}
        self.buf = {}
        self.dsem = {}
        self.dcnt = {}
        self.semobj = {}
        for e in self.CE:
            self.semobj[id(self.sem[e])] = self.sem[e]
        self.nwaits = 0

    def _b(self, k):
        b = self.buf.get(k)
        if b is None:
            b = self.buf[k] = {'w': {}, 'r': {}}
        return b

    def _emit_waits(self, q, toks):
        need = {}
        for t in toks:
            if t is None:
                continue
            sid, v = t
            if self.seen[q].get(sid, 0) >= v:
                continue
            if need.get(sid, 0) < v:
                need[sid] = v
        for sid, v in need.items():
            self.seen[q][sid] = v
            s = self.semobj[sid]
            self.ops[q].append(lambda E, s=s, v=v: E.wait_ge(s, v))
            self.nwaits += 1

    def _deps(self, q, r, w):
        toks = []
        own = id(self.sem[q]) if q == 'pe' else None
        for k in r:
            b = self._b(k)
            toks.extend(b['w'].items())
        for k in w:
            b = self._b(k)
            toks.extend(b['w'].items())
            toks.extend(b['r'].items())
        toks = [t for t in toks if t is not None and t[0] != own]
        self._emit_waits(q, toks)

    def _commit(self, tok, r, w, acc=False):
        for k in r:
            b = self._b(k)
            if b['r'].get(tok[0], 0) < tok[1]:
                b['r'][tok[0]] = tok[1]
        for k in w:
            if acc:
                b = self._b(k)
                b['w'][tok[0]] = max(b['w'].get(tok[0], 0), tok[1])
            else:
                self.buf[k] = {'w': {tok[0]: tok[1]}, 'r': {}}

    def op(self, q, fn, r=(), w=()):
        fn = _freeze(fn)
        self._deps(q, r, w)
        self.cnt[q] += 1
        s = self.sem[q]
        self.ops[q].append(lambda E, fn=fn, s=s: fn(E).then_inc(s, 1))
        self._commit((id(s), self.cnt[q]), r, w)

    def dma(self, q, out, in_, r=(), w=(), semkey=None, acc=False, **kw):
        if acc:
            self._deps(q, r, ())
            toks = []
            for k in w:
                toks.extend(self._b(k)['r'].items())
            self._emit_waits(q, toks)
        else:
            self._deps(q, r, w)
        k0 = semkey if semkey is not None else w[0]
        if k0 not in self.dsem:
            s = self.es.enter_context(self.nc.semaphore('d%d' % len(self.dsem)))
            self.dsem[k0] = s
            self.dcnt[k0] = 0
            self.semobj[id(s)] = s
        s = self.dsem[k0]
        self.dcnt[k0] += 16
        self.ops[q].append(lambda E, s=s, out=out, in_=in_, kw=kw: E.dma_start(out=out, in_=in_, **kw).then_inc(s, 16))
        self._commit((id(s), self.dcnt[k0]), r, w, acc)

    def alias_sem(self, knew, kold):
        raise NotImplementedError

    def barrier(self):
        toks = [(id(self.sem[e]), self.cnt[e]) for e in self.CE]
        toks += [(id(self.dsem[k]), self.dcnt[k]) for k in self.dsem]
        for q in self.ALLQ:
            own = id(self.sem[q]) if q in self.sem else None
            self._emit_waits(q, [t for t in toks if t[0] != own and t[1] > 0])
        self.buf = {}

    def finish(self):
        self.barrier()
        nc = self.nc
        ops = self.ops
        block = self.es.enter_context(nc.Block())

        @block.tensor
        def _(E):
            for f in ops['pe']:
                f(E)

        @block.scalar
        def _(E):
            for f in ops['act']:
                f(E)

        @block.vector
        def _(E):
            for f in ops['dve']:
                f(E)

        @block.gpsimd
        def _(E):
            for f in ops['pool']:
                f(E)

        @block.sync
        def _(E):
            for f in ops['sp']:
                f(E)


def na_key_tiles(t):
    rows = 32
    qr = [2 * t, 2 * t + 1]
    rs = [int(np.clip(r - 4, 0, rows - 8)) for r in qr]
    lo = min(rs); hi = max(rs) + 8
    us = list(range(lo // 2, (hi + 1) // 2))
    return us


def build_na_tables(rpb):
    rows = 32
    ids = {}
    pats = []
    for t in range(16):
        for u in na_key_tiles(t):
            qr = np.array([2 * t, 2 * t + 1])
            kr = np.array([2 * u, 2 * u + 1])
            rs = np.clip(qr - 4, 0, rows - 8)
            rvalid = (kr[:, None] >= rs[None, :]) & (kr[:, None] < rs[None, :] + 8)
            ridx = kr[:, None] - qr[None, :] + 7
            key = (tuple(rvalid.ravel().tolist()), tuple(ridx.ravel().tolist()))
            if key not in pats:
                pats.append(key)
            ids[(t, u)] = pats.index(key)
    qc = np.arange(64); kc = np.arange(64)
    cs = np.clip(qc - 8, 0, 64 - 16)
    cvalid = (kc[:, None] >= cs[None, :]) & (kc[:, None] < cs[None, :] + 16)
    cidx = np.clip(kc[:, None] - qc[None, :] + 15, 0, 30)
    tabs = np.full((16, len(pats), 128, 128), NEG, np.float32)
    for pi, (rv, ri) in enumerate(pats):
        rv = np.array(rv).reshape(2, 2); ri = np.array(ri).reshape(2, 2)
        for krl in range(2):
            for qrl in range(2):
                if not rv[krl, qrl]:
                    continue
                g = rpb[:, int(np.clip(ri[krl, qrl], 0, 14)), :][:, cidx]
                blk = np.where(cvalid[None], g, np.float32(NEG))
                tabs[:, pi, krl * 64:(krl + 1) * 64, qrl * 64:(qrl + 1) * 64] = blk
    return tabs, ids


_NA_IDS = None


def na_ids():
    global _NA_IDS
    if _NA_IDS is None:
        _, _NA_IDS = build_na_tables(np.zeros((16, 15, 31), np.float32))
    return _NA_IDS


NB_NA = 9


def host_consts():
    i = np.arange(128)
    k = i[:, None]; j = i[None, :]
    c = np.zeros((NCST, 128, 128), np.float32)
    c[C_ID] = (k == j)
    c[C_MLE] = (k <= j)
    c[C_NMLT] = -(k < j).astype(np.float32)
    c[C_U] = (k > j)
    c[C_LW] = (k < j)
    c[C_ONES] = 1.0
    c[C_NMF] = NEG * (k > j)
    c[C_NMB] = NEG * (k < j)
    c[C_MF01] = (k <= j)
    c[C_MB01] = (k >= j)
    c[C_SWLO] = NEG * (k < j)
    c[C_SWHI] = NEG * (k > j)
    c[C_NMLE] = -(k <= j).astype(np.float32)
    return c


def host_rope():
    pos = np.arange(L)
    inv = (10000.0 ** (-np.arange(16, dtype=np.float32) / 16)).astype(np.float32)
    cos = np.zeros((64, L), np.float32); sin = np.zeros((64, L), np.float32)
    for half, p in enumerate([pos // 64, pos % 64]):
        ang = p.astype(np.float32)[None, :] * inv[:, None]
        cs = np.cos(ang).astype(np.float32); sn = np.sin(ang).astype(np.float32)
        b = half * 32
        cos[b:b + 16] = cs; cos[b + 16:b + 32] = cs
        sin[b:b + 16] = -sn; sin[b + 16:b + 32] = sn
    return np.concatenate([cos, cos], 0), np.concatenate([sin, sin], 0)


def build(nc, n_layers=DEPTH, dbg=False, stop_after=None):
    es = ExitStack()
    S = Sched(nc, es)
    sk = "ExternalOutput" if dbg else "Internal"

    def din(name, shape, dt=F32):
        return nc.dram_tensor(name, list(shape), dt, kind="ExternalInput").ap()

    def dscr(name, shape, dt=BF16):
        return nc.dram_tensor(name, list(shape), dt, kind=sk).ap()

    x_in = din("x", [L, D]); ctx_in = din("ctx", [NCTX, D]); cc_in = din("cc", [128, KC, 2])
    ada_w = din("ada_w", [DEPTH, D, 6 * D]); ada_b2 = din("ada_b2", [DEPTH, 2, 6 * D])
    n1g_in = din("n1g", [DEPTH, 128, KC]); n2g_in = din("n2g", [DEPTH, 128, KC]); fing_in = din("final_g", [1, D])
    w_in = din("w_in", [DEPTH, D, IN_COLS])
    cw_in = din("cw", [DEPTH, 128, 32, 5]); cb_in = din("cb", [DEPTH, 128, 32])
    dtb_in = din("dtb", [DEPTH, 1, 64]); alog_in = din("alog", [DEPTH, 1, 64]); ssdd_in = din("ssdd", [DEPTH, 1, 32])
    sng_in = din("sng", [DEPTH, 128, 16])
    nab_in = din("nab", [DEPTH, 16, NB_NA, 128, 128]); sink_in = din("sink", [DEPTH, 1, 16])
    w_o_ssd = din("w_o_ssd", [DEPTH, 2048, D]); w_o_na = din("w_o_na", [DEPTH, D, D]); w_o_swa = din("w_o_swa", [DEPTH, D, D])
    w_out = din("w_out", [DEPTH, D, D]); w_ff1 = din("w_ff1", [DEPTH, D, 4 * D]); w_ff2 = din("w_ff2", [DEPTH, 4 * D, D])
    cst_in = din("cst", [NCST, 128, 128]); ropec_in = din("ropec", [128, L]); ropes_in = din("ropes", [128, L])
    out_d = nc.dram_tensor("out", [L, D], F32, kind="ExternalOutput").ap()

    xres = dscr("xres", [T, D], F32)
    XBM = dscr("XBM", [T, 3072]); BCT = dscr("BCT", [2048, T]); ZTM = dscr("ZTM", [T, 2048])
    NAK = dscr("NAK", [D, T]); NAQ = dscr("NAQ", [D, T]); NAV = dscr("NAV", [T, D])
    SWK = dscr("SWK", [256, T]); SWQ = dscr("SWQ", [D, T]); SWV = dscr("SWV", [T, 256])
    GFM = dscr("GFM", [3072, T]); SBST = dscr("SBST", [NT, 128, 2048])

    uid = [0]

    def sb(name, shape, dt=F32, stack=None):
        uid[0] += 1
        return (stack or es).enter_context(nc.sbuf_tensor('s%d_%s' % (uid[0], name), list(shape), dt))

    psall = es.enter_context(nc.psum_tensor("psall", [128, 4096], F32))

    def PS(b, off=0, n=512):
        return psall[:, b * 512 + off:b * 512 + off + n]

    def PSB(b):
        return psall[:, b * 512:(b + 1) * 512].bitcast(BF16)

    def PK(*banks):
        ks = []
        for b in banks:
            ks.append('ps%d' % b)
            if b < 2:
                ks += ['pe%d' % r for r in range(4 * b, 4 * b + 4)]
        return ks

    def bc3(ap, n):
        return ap.unsqueeze(2).broadcast_to([ap.shape[0], ap.shape[1], n])

    def v3(ap, d):
        return ap.rearrange("p (h d) -> p h d", d=d)

    cst = sb("cst", [128, NCST, 128], F32)
    cstb = sb("cstb", [128, NCST, 128], BF16)
    identf = cst[:, C_ID, :]
    identb = cstb[:, C_ID, :]
    cc = sb("ccs", [128, KC, 2], F32)
    sc2 = sb("sc2", [128, KC, 2], BF16)
    fing = sb("fing", [128, D], F32)
    modT = sb("modT", [128, 48, 2], F32)
    G1 = sb("G1", [128, KC, 2], F32); S1 = sb("S1", [128, KC, 2], F32)
    G2 = sb("G2", [128, KC, 2], F32); S2 = sb("S2", [128, KC, 2], F32)
    gbc = sb("gbc", [128, 2, 2, D], F32)
    sel = sb("sel", [2, 2, 128], F32)
    ngam = sb("ngam", [128, 2, KC], F32)
    dt_all = sb("dt_all", [128, NT, 64], F32)
    A2 = sb("A2", [128, 64], F32); dtb2 = sb("dtb2", [128, 64], F32); Dbc = sb("Dbc", [128, 32], F32)
    sinkexp = sb("sinkexp", [128, 16], F32)
    epsb = sb("epsb", [128, 1], F32)

    q_sp = 'sp'
    q_cast = 'pool'

    for i in range(NCST):
        S.dma(q_sp, cst[:, i, :], cst_in[i], w=['cst'])
    S.op('act', lambda E: E.activation(out=cstb[:], in_=cst[:], func=AF.Copy), r=['cst'], w=['cstb'])
    S.dma(q_sp, cc[:], cc_in[:, :, :], w=['cc'])
    S.dma(q_sp, fing[:], fing_in[0:1, :].partition_broadcast(128), w=['fing'])
    S.op('act', lambda E: E.activation(out=sc2[:], in_=cc[:], func=AF.Silu), r=['cc'], w=['sc2'])
    S.op('pool', lambda E: E.memset(sel[:], 0.0), w=['sel'])
    S.op('pool', lambda E: E.memset(sel[0:1, 0, :], 1.0), w=['sel'])
    S.dma(q_sp, sel[1:2, 1, :], cst_in[C_ONES, 0:1, :], r=[], w=['sel'])
    S.op('pool', lambda E: E.memset(epsb[:], EPS), w=['epsb'])

    def tok_type(tt):
        return 1 if tt < 2 else 0

    def xsrc(l, tt):
        if l == 0:
            return ctx_in[tt * 128:(tt + 1) * 128, :] if tt < 2 else x_in[(tt - 2) * 128:(tt - 1) * 128, :]
        return xres[tt * 128:(tt + 1) * 128, :]

    def cast_load(dst, src, key, rows_per=None):
        S.dma(q_cast, dst, src, w=[key])

    def norm_to_T(l, tt, xt, xkey, G, Sh, hT, hkey, col0, wk):
        junk, ss, xn = wk['junk'], wk['ss'], wk['xn']
        tk = tok_type(tt)
        S.op('act', lambda E: E.activation(out=junk[:], in_=xt, func=AF.Square, accum_out=ss[:]), r=[xkey], w=['junk', 'ss'])
        S.op('dve', lambda E: E.tensor_scalar(out=ss[:], in0=ss[:], scalar1=1.0 / D, scalar2=EPS, op0=ALU.mult, op1=ALU.add), r=['ss'], w=['ss'])
        S.op('act', lambda E: E.activation(out=ss[:], in_=ss[:], func=AF.Sqrt), r=['ss'], w=['ss'])
        S.op('dve', lambda E: E.reciprocal(out=ss[:], in_=ss[:]), r=['ss'], w=['ss'])
        S.op('dve', lambda E: E.tensor_scalar(out=xn[:], in0=xt, scalar1=ss[:, 0:1], scalar2=None, op0=ALU.mult), r=[xkey, 'ss'], w=['xn'])
        pb = PSB(5)
        for k in range(KC):
            S.op('pe', lambda E, k=k: E.transpose(out=pb[:, k * 128:(k + 1) * 128], in_=xn[:, k * 128:(k + 1) * 128], identity=identb),
                 r=['xn', 'cstb'], w=['ps5'])
        for k in range(KC):
            S.op('act', lambda E, k=k: E.activation(out=hT[:, k, col0:col0 + 128], in_=pb[:, k * 128:(k + 1) * 128], func=AF.Identity,
                                                     scale=G[:, k, tk:tk + 1], bias=Sh[:, k, tk:tk + 1]),
                 r=['ps5', 'GS'], w=[hkey])

    def phase_adaln(l):
        with ExitStack() as ps_:
            wa = [sb("adaw%d" % i, [128, KC, 512], BF16, ps_) for i in range(2)]
            m2 = sb("m2", [2, 512], F32, ps_)
            ab = sb("ab", [2, 6 * D], F32, ps_)
            n1 = sb("n1", [128, KC], F32, ps_); n2 = sb("n2", [128, KC], F32, ps_)
            S.dma(q_sp, ab[:], ada_b2[l], w=['ab'])
            S.dma(q_sp, n1[:], n1g_in[l], w=['n1']); S.dma(q_sp, n2[:], n2g_in[l], w=['n2'])
            for g in range(12):
                w = wa[g % 2]; wk = 'adaw%d' % (g % 2)
                cast_load(w[:], ada_w[l, :, g * 512:(g + 1) * 512].rearrange("(k p) n -> p k n", p=128), wk)
                for k in range(KC):
                    S.op('pe', lambda E, k=k, w=w: E.matmul(PS(0)[0:2, :], lhsT=sc2[:, k, :], rhs=w[:, k, :], start=(k == 0), stop=(k == KC - 1)),
                         r=['sc2', wk], w=['ps0'])
                S.op('dve', lambda E, g=g: E.tensor_tensor(out=m2[:], in0=PS(0)[0:2, :], in1=ab[:, g * 512:(g + 1) * 512], op=ALU.add),
                     r=['ps0', 'ab'], w=['m2'])
                for j in range(4):
                    jj = g * 4 + j
                    S.op('pe', lambda E, j=j, jj=jj: E.transpose(out=PS(1)[:, jj * 2:jj * 2 + 2], in_=m2[0:2, j * 128:(j + 1) * 128], identity=identf[0:2, 0:2]),
                         r=['m2', 'cst'], w=['ps1'])
                if g in (4, 5, 10, 11):
                    gi = 0 if g < 6 else 1
                    half = g % 2
                    for t in range(2):
                        S.op('pe', lambda E, t=t: E.matmul(PS(2 + t), lhsT=sel[0:2, t, :], rhs=m2[0:2, :], start=True, stop=True),
                             r=['m2', 'sel'], w=['ps%d' % (2 + t)])
                        S.op('act', lambda E, t=t, gi=gi, half=half: E.activation(out=gbc[:, t, gi, half * 512:(half + 1) * 512], in_=PS(2 + t), func=AF.Copy),
                             r=['ps%d' % (2 + t)], w=['gbc'])
            S.op('dve', lambda E: E.tensor_copy(out=modT[:].rearrange("p j t -> p (j t)"), in_=PS(1)[:, 0:96]), r=['ps1'], w=['modT'])
            for (G, Sh, ng, js, jsh) in ((G1, S1, n1, 8, 0), (G2, S2, n2, 32, 24)):
                for t in range(2):
                    S.op('dve', lambda E, G=G, ng=ng, js=js, t=t: E.scalar_tensor_tensor(out=G[:, :, t], in0=modT[:, js:js + 8, t], scalar=1.0, in1=ng[:],
                                                                                         op0=ALU.add, op1=ALU.mult), r=['modT', 'n1', 'n2'], w=['GS'])
                    S.op('dve', lambda E, Sh=Sh, jsh=jsh, t=t: E.tensor_copy(out=Sh[:, :, t], in_=modT[:, jsh:jsh + 8, t]), r=['modT'], w=['GS'])
            S.dma(q_sp, dtb2[:], dtb_in[l].partition_broadcast(128), w=['dtb2'])
            S.dma(q_sp, A2[:], alog_in[l].partition_broadcast(128), w=['A2'])
            S.dma(q_sp, Dbc[:], ssdd_in[l].partition_broadcast(128), w=['Dbc'])
            S.dma(q_sp, sinkexp[:], sink_in[l].partition_broadcast(128), w=['sinkexp'])
            S.op('act', lambda E: E.activation(out=A2[:], in_=A2[:], func=AF.Exp), r=['A2'], w=['A2'])
            S.op('dve', lambda E: E.tensor_scalar(out=A2[:], in0=A2[:], scalar1=-1.0, scalar2=None, op0=ALU.mult), r=['A2'], w=['A2'])
            S.op('act', lambda E: E.activation(out=sinkexp[:], in_=sinkexp[:], func=AF.Exp), r=['sinkexp'], w=['sinkexp'])
            S.barrier()

    def phase_proj(l):
        last = (l == n_layers - 1) and (n_layers == DEPTH)
        with ExitStack() as ps_:
            hT = sb("hT", [128, KC, T], BF16, ps_)
            wk = dict(junk=sb("junk", [128, D], BF16, ps_), ss=sb("ss", [128, 1], F32, ps_), xn=sb("xn", [128, D], BF16, ps_))
            xt2 = [sb("xt%d" % i, [128, D], F32, ps_) for i in range(2)]
            for tt in range(NT):
                xt = xt2[tt % 2]; xk = 'xt%d' % (tt % 2)
                S.dma(q_sp, xt[:], xsrc(l, tt), w=[xk])
                if l == 0:
                    S.dma(q_sp, xres[tt * 128:(tt + 1) * 128, :], xt[:], r=[xk], w=['xres'])
                norm_to_T(l, tt, xt[:], xk, G1, S1, hT, 'hT', tt * 128, wk)
            wb = [sb("wb%d" % i, [128, KC, 512], BF16, ps_) for i in range(3)]
            wsw = sb("wsw", [128, KC, 512], BF16, ps_)
            stg = [sb("stg%d" % i, [128, 512], BF16, ps_) for i in range(4)]
            ropec = sb("ropec", [128, L], F32, ps_); ropes = sb("ropes", [128, L], F32, ps_)
            rt1 = sb("rt1", [128, 512], F32, ps_); rt2 = sb("rt2", [128, 512], F32, ps_)
            cw = sb("cw", [128, 32, 5], F32, ps_); cbv = sb("cbv", [128, 32], F32, ps_)
            dtr = sb("dtr", [128, 64], F32, ps_)
            conv_scope = ExitStack()
            cvins = [sb("cvin%d" % i, [128, 2312], F32, conv_scope) for i in range(2)]; cvaccs = [sb("cvacc%d" % i, [128, 2308], F32, conv_scope) for i in range(2)]
            cvos = [sb("cvo%d" % i, [128, 2308], BF16, conv_scope) for i in range(2)]; tmst = sb("tmst", [128, NT, 128], BF16, conv_scope)
            S.dma(q_sp, ropec[:], ropec_in[:, :], w=['ropec']); S.dma(q_sp, ropes[:], ropes_in[:, :], w=['ropes'])
            S.dma(q_sp, cw[:], cw_in[l], w=['cw']); S.dma(q_sp, cbv[:], cb_in[l], w=['cbv'])
            for i in range(2):
                S.op('pool', lambda E, i=i: E.memset(cvins[i][:], 0.0), w=['cvin%d' % i])
            state = {'wi': 0, 'si': 0, 'pi': 0}

            def load_w(c0, ncols):
                i = state['wi'] % 3; state['wi'] += 1
                w = wb[i]
                cast_load(w[:, :, 0:ncols], w_in[l, :, c0:c0 + ncols].rearrange("(k p) n -> p k n", p=128), 'wb%d' % i)
                return w, 'wb%d' % i

            def nbank():
                b = state['pi'] % 4; state['pi'] += 1
                return b

            def nstg():
                i = state['si'] % 4; state['si'] += 1
                return stg[i], 'stg%d' % i

            def mm_fm(w, wkey, j, blk, bank):
                t0, n = BLKS[blk]
                for k in range(KC):
                    S.op('pe', lambda E, k=k: E.matmul(PS(bank, 0, n), lhsT=w[:, k, j * 128:(j + 1) * 128], rhs=hT[:, k, t0:t0 + n],
                                                       start=(k == 0), stop=(k == KC - 1)), r=[wkey, 'hT'], w=['ps%d' % bank])

            def mm_tm(w, wkey, ncols, tt, bank):
                for k in range(KC):
                    S.op('pe', lambda E, k=k: E.matmul(PS(bank, 0, ncols), lhsT=hT[:, k, tt * 128:(tt + 1) * 128], rhs=w[:, k, 0:ncols],
                                                       start=(k == 0), stop=(k == KC - 1)), r=[wkey, 'hT'], w=['ps%d' % bank])

            def fm_simple(c0, ngroups_cols, dst, dkey, func, scale=1.0, skip_ctx=False):
                ncols = ngroups_cols
                for g0 in range(0, ncols, 512):
                    nc_ = min(512, ncols - g0)
                    w, wkey = load_w(c0 + g0, nc_)
                    for j in range(nc_ // 128):
                        row0 = g0 + j * 128
                        for blk in range(5):
                            if skip_ctx and blk == 0:
                                continue
                            t0, n = BLKS[blk]
                            bank = nbank()
                            mm_fm(w, wkey, j, blk, bank)
                            st, sk_ = nstg()
                            S.op('act', lambda E, st=st, bank=bank, n=n: E.activation(out=st[:, 0:n], in_=PS(bank, 0, n), func=func, scale=scale),
                                 r=['ps%d' % bank], w=[sk_])
                            S.dma(q_sp, dst[row0:row0 + 128, t0:t0 + n], st[:, 0:n], r=[sk_], w=[dkey], semkey='o' + sk_, acc=True)
                        yield

            def tm_simple(c0, ncols, dst, dkey, func):
                for g0 in range(0, ncols, 512):
                    nc_ = min(512, ncols - g0)
                    w, wkey = load_w(c0 + g0, nc_)
                    for tt in range(NT):
                        bank = nbank()
                        mm_tm(w, wkey, nc_, tt, bank)
                        st, sk_ = nstg()
                        S.op('act', lambda E, st=st, bank=bank: E.activation(out=st[:, 0:nc_], in_=PS(bank, 0, nc_), func=func), r=['ps%d' % bank], w=[sk_])
                        S.dma(q_sp, dst[tt * 128:(tt + 1) * 128, g0:g0 + nc_], st[:, 0:nc_], r=[sk_], w=[dkey], semkey='o' + sk_, acc=True)
                        if tt % 4 == 3:
                            yield

            def fm_rope(c0, ncols, dst, dkey, scale):
                for g0 in range(0, ncols, 512):
                    nc_ = min(512, ncols - g0)
                    w, wkey = load_w(c0 + g0, nc_)
                    wv = w[:, :, 0:nc_].rearrange("p k (j two s) -> p k j two s", two=2, s=16)
                    sv = wsw[:, :, 0:nc_].rearrange("p k (j two s) -> p k j two s", two=2, s=16)
                    for k in range(KC):
                        S.op('pool', lambda E, k=k: E.tensor_copy(out=sv[:, k, :, 0, :], in_=wv[:, k, :, 1, :]), r=[wkey], w=['wsw'])
                        S.op('pool', lambda E, k=k: E.tensor_copy(out=sv[:, k, :, 1, :], in_=wv[:, k, :, 0, :]), r=[wkey], w=['wsw'])
                    for j in range(nc_ // 128):
                        row0 = g0 + j * 128
                        for blk in range(5):
                            t0, n = BLKS[blk]
                            bank = nbank()
                            mm_fm(w, wkey, j, blk, bank)
                            st, sk_ = nstg()
                            if blk == 0:
                                S.op('act', lambda E, st=st, bank=bank, n=n: E.activation(out=st[:, 0:n], in_=PS(bank, 0, n), func=AF.Copy, scale=scale),
                                     r=['ps%d' % bank], w=[sk_])
                            else:
                                bank2 = nbank()
                                mm_fm(wsw, 'wsw', j, blk, bank2)
                                l0 = t0 - 256
                                S.op('dve', lambda E, bank=bank, l0=l0: E.tensor_tensor(out=rt1[:], in0=PS(bank), in1=ropec[:, l0:l0 + 512], op=ALU.mult),
                                     r=['ps%d' % bank, 'ropec'], w=['rt1'])
                                S.op('dve', lambda E, bank2=bank2, l0=l0: E.tensor_tensor(out=rt2[:], in0=PS(bank2), in1=ropes[:, l0:l0 + 512], op=ALU.mult),
                                     r=['ps%d' % bank2, 'ropes'], w=['rt2'])
                                S.op('pool', lambda E: E.tensor_tensor(out=rt1[:], in0=rt1[:], in1=rt2[:], op=ALU.add), r=['rt1', 'rt2'], w=['rt1'])
                                S.op('act', lambda E, st=st: E.activation(out=st[:], in_=rt1[:], func=AF.Copy, scale=scale), r=['rt1'], w=[sk_])
                            S.dma(q_sp, dst[row0:row0 + 128, t0:t0 + n], st[:, 0:n], r=[sk_], w=[dkey], semkey='o' + sk_, acc=True)
                        yield

            def conv_post(ct):
                cp = ct % 2
                cvo = cvos[cp]; ko = 'cvo%d' % cp
                if ct >= 16:
                    r0 = (ct - 16) * 128
                    S.dma(q_sp, BCT[r0:r0 + 128, 0:256], cvo[:, 0:256], r=[ko], w=['BCT'], semkey='oBCT%d' % cp, acc=True)
                    S.dma(q_sp, BCT[r0:r0 + 128, 256:T], cvo[:, 260:2308], r=[ko], w=['BCT'], semkey='oBCT%d' % cp, acc=True)
                if ct < 24:
                    for tt in range(NT):
                        o0 = tt * 128 if tt < 2 else 260 + (tt - 2) * 128
                        bnk = 4 + (tt // 8) % 2 if tt < 16 else 6
                        off = (tt % 8) * 128
                        S.op('pe', lambda E, o0=o0, bnk=bnk, off=off: E.transpose(out=PSB(bnk)[:, off:off + 128], in_=cvo[:, o0:o0 + 128], identity=identb),
                             r=[ko, 'cstb'], w=['ps%d' % bnk])
                    for gi, (bnk, n8) in enumerate(((4, 8), (5, 8), (6, 2))):
                        S.op('dve', lambda E, gi=gi, bnk=bnk, n8=n8: E.tensor_copy(out=tmst[:, gi * 8:gi * 8 + n8, :].rearrange("p a b -> p (a b)"),
                                                                                   in_=PSB(bnk)[:, 0:n8 * 128]), r=['ps%d' % bnk], w=['tmst'])
                    S.dma(q_sp, XBM[:, ct * 128:(ct + 1) * 128].rearrange("(t p) c -> p t c", p=128), tmst[:], r=['tmst'], w=['XBM'])

            for g in range(8):
                w, wkey = load_w(XBC_OFF + g * 512, 512)
                for j in range(4):
                    ct = g * 4 + j
                    cp = ct % 2
                    cvin = cvins[cp]; cvacc = cvaccs[cp]; cvo = cvos[cp]
                    kin = 'cvin%d' % cp; kacc = 'cvacc%d' % cp; ko = 'cvo%d' % cp
                    for blk in range(5):
                        t0, n = BLKS[blk]
                        bank = nbank()
                        mm_fm(w, wkey, j, blk, bank)
                        o0 = 2 if blk == 0 else 262 + (t0 - 256)
                        S.op('act', lambda E, bank=bank, n=n, o0=o0: E.activation(out=cvin[:, o0:o0 + n], in_=PS(bank, 0, n), func=AF.Copy),
                             r=['ps%d' % bank], w=[kin])
                    if ct > 0:
                        conv_post(ct - 1)
                    S.op('dve', lambda E, ct=ct: E.tensor_scalar(out=cvacc[:], in0=cvin[:, 0:2308], scalar1=cw[:, ct, 0:1], scalar2=None, op0=ALU.mult),
                         r=[kin, 'cw'], w=[kacc])
                    for kk in range(1, 5):
                        S.op('dve', lambda E, ct=ct, kk=kk: E.scalar_tensor_tensor(out=cvacc[:], in0=cvin[:, kk:kk + 2308], scalar=cw[:, ct, kk:kk + 1],
                                                                                    in1=cvacc[:], op0=ALU.mult, op1=ALU.add), r=[kin, 'cw'], w=[kacc])
                    S.op('act', lambda E, ct=ct: E.activation(out=cvo[:], in_=cvacc[:], func=AF.Silu, bias=cbv[:, ct:ct + 1]), r=[kacc, 'cbv'], w=[ko])
            conv_post(31)
            w, wkey = load_w(DT_OFF, 64)
            for tt in range(NT):
                bank = nbank()
                mm_tm(w, wkey, 64, tt, bank)
                S.op('dve', lambda E, bank=bank: E.tensor_tensor(out=dtr[:], in0=PS(bank, 0, 64), in1=dtb2[:], op=ALU.add), r=['ps%d' % bank, 'dtb2'], w=['dtr'])
                S.op('act', lambda E: E.activation(out=dtr[:], in_=dtr[:], func=AF.Exp), r=['dtr'], w=['dtr'])
                S.op('act', lambda E, tt=tt: E.activation(out=dt_all[:, tt, :], in_=dtr[:], func=AF.Ln, bias=1.0), r=['dtr'], w=['dt_all'])
            S.barrier()
            conv_scope.close()
            bwk = ssd_wk(ps_, '_b'); bwk['sbank'] = 4
            bxB = [sb("bxB%d" % i, [128, 3072], BF16, ps_) for i in range(2)]
            bxw = sb("bxw", [128, 2048], BF16, ps_)
            bst = sb("bst", [128, 2048], F32, ps_); bst_bf = sb("bst_bf", [128, 2048], BF16, ps_); btmp = sb("btmp", [128, 1024], F32, ps_)

            def bwd_gen():
                S.op('pool', lambda E: E.memset(bst[:], 0.0), w=['st'])
                S.op('pool', lambda E: E.memset(bst_bf[:], 0.0), w=['st_bf'])
                order = [1, 0] + list(range(17, 1, -1))
                for i, tt in enumerate(order):
                    xB = bxB[i % 2]; xk = 'bxB%d' % (i % 2)
                    S.dma(q_sp, SBST[tt], bst_bf[:], r=['st_bf'], w=['SBST'], semkey='oSB', acc=True)
                    if i == len(order) - 1:
                        break
                    S.dma(q_sp, xB[:], XBM[tt * 128:(tt + 1) * 128, :], w=[xk])
                    yield
                    yield from ssd_small(tt, bwk)
                    yield
                    yield from ssd_state_update(xB, xk, bxw, 32, 160, bst, bst_bf, btmp, bwk)
                    yield

            def proj_rest():
                yield from fm_simple(NA_K_OFF, 1024, NAK, 'NAK', AF.Copy)
                yield from tm_simple(NA_V_OFF, 1024, NAV, 'NAV', AF.Copy)
                yield from fm_rope(SWA_K_OFF, 256, SWK, 'SWK', 1.0)
                yield from tm_simple(SWA_V_OFF, 256, SWV, 'SWV', AF.Copy)
                yield from tm_simple(Z_OFF, 2048, ZTM, 'ZTM', AF.Silu)
                yield from fm_simple(NA_Q_OFF, 1024, NAQ, 'NAQ', AF.Copy, scale=0.125)
                yield from fm_rope(SWA_Q_OFF, 1024, SWQ, 'SWQ', 0.125)
                yield from fm_simple(GATE_OFF, 3072, GFM, 'GFM', AF.Sigmoid)

            gens = [proj_rest(), bwd_gen()]
            while gens:
                for g_ in list(gens):
                    try:
                        next(g_)
                    except StopIteration:
                        gens.remove(g_)
            S.barrier()


    def ssd_small(tt, wk):
        a2, sg, expS, tmp64, e64, bias64, w64 = (wk[k] for k in ('a2', 'sg', 'expS', 'tmp64', 'e64', 'bias64', 'w64'))
        x = wk.get('sfx', '')
        sbk = wk.get('sbank', 0)
        S.op('dve', lambda E: E.tensor_tensor(out=a2[:], in0=dt_all[:, tt, :], in1=A2[:], op=ALU.mult), r=['dt_all', 'A2'], w=['a2' + x])
        yield
        for i, ci in enumerate((C_U, C_LW, C_ONES)):
            S.op('pe', lambda E, i=i, ci=ci: E.matmul(PS(sbk, i * 64, 64), lhsT=cst[:, ci, :], rhs=a2[:], start=True, stop=True), r=['a2' + x, 'cst'], w=PK(sbk))
        S.op('act', lambda E: E.activation(out=sg[:], in_=PS(sbk, 0, 192), func=AF.Copy), r=PK(sbk), w=['sg' + x])
        S.op('act', lambda E: E.activation(out=expS[:], in_=sg[:], func=AF.Exp), r=['sg' + x], w=['expS' + x])
        S.op('dve', lambda E: E.tensor_tensor(out=tmp64[:, 0:32], in0=sg[:, 128:160], in1=sg[:, 0:32], op=ALU.subtract), r=['sg' + x], w=['tmp64' + x])
        S.op('dve', lambda E: E.tensor_tensor(out=tmp64[:, 32:64], in0=sg[:, 160:192], in1=sg[:, 96:128], op=ALU.subtract), r=['sg' + x], w=['tmp64' + x])
        S.op('act', lambda E: E.activation(out=e64[:], in_=tmp64[:], func=AF.Exp), r=['tmp64' + x], w=['e64' + x])
        S.op('dve', lambda E: E.tensor_tensor(out=w64[:, 0:32], in0=expS[:, 0:32], in1=dt_all[:, tt, 0:32], op=ALU.mult), r=['expS' + x, 'dt_all'], w=['w64' + x])
        S.op('dve', lambda E: E.tensor_tensor(out=w64[:, 32:64], in0=expS[:, 96:128], in1=dt_all[:, tt, 32:64], op=ALU.mult), r=['expS' + x, 'dt_all'], w=['w64' + x])

    def ssd_state_update(xB, xbkey, xw, wcol, cdcol, st, st_bf, tmpY, wk):
        w64, expS = wk['w64'], wk['expS']
        x = wk.get('sfx', '')
        S.op('dve', lambda E: E.tensor_tensor(out=v3(xw[:], 64), in0=v3(xB[:, 0:2048], 64), in1=bc3(w64[:, wcol:wcol + 32], 64), op=ALU.mult),
             r=[xbkey, 'w64' + x], w=['xw'])
        yield
        for half in range(2):
            for gg in range(4):
                g = half * 4 + gg
                S.op('pe', lambda E, g=g, gg=gg: E.matmul(PS(6 + gg // 2, (gg % 2) * 256, 256), lhsT=xB[:, 2048 + g * 128:2048 + (g + 1) * 128],
                                                           rhs=xw[:, g * 256:(g + 1) * 256], start=True, stop=True), r=[xbkey, 'xw'], w=PK(6 + gg // 2))
            c0 = half * 1024
            S.op('pool', lambda E, c0=c0, half=half: E.tensor_tensor(out=v3(tmpY[:, 0:1024], 64), in0=v3(st[:, c0:c0 + 1024], 64),
                                                                    in1=bc3(expS[:, cdcol + half * 16:cdcol + half * 16 + 16], 64), op=ALU.mult),
                 r=['st', 'expS' + x], w=['tmpSU'])
            S.op('dve', lambda E, c0=c0: E.tensor_tensor(out=st[:, c0:c0 + 1024], in0=psall[:, 6 * 512:8 * 512], in1=tmpY[:, 0:1024], op=ALU.add),
                 r=PK(6, 7) + ['tmpSU'], w=['st'])
        S.op('act', lambda E: E.activation(out=st_bf[:], in_=st[:], func=AF.Copy), r=['st'], w=['st_bf'])

    def ssd_wk(ps_, sfx=''):
        return dict(sfx=sfx, a2=sb("a2", [128, 64], F32, ps_), sg=sb("sg", [128, 192], F32, ps_), expS=sb("expS", [128, 192], F32, ps_),
                    tmp64=sb("tmp64", [128, 64], F32, ps_), e64=sb("e64", [128, 64], F32, ps_), bias64=sb("bias64", [128, 64], F32, ps_),
                    w64=sb("w64", [128, 64], F32, ps_))

    def phase_ssd_bwd(l):
        with ExitStack() as ps_:
            wk = ssd_wk(ps_)
            xBs = [sb("xB%d" % i, [128, 3072], BF16, ps_) for i in range(2)]
            xw = sb("xw", [128, 2048], BF16, ps_)
            st = sb("st", [128, 2048], F32, ps_); st_bf = sb("st_bf", [128, 2048], BF16, ps_); tmpY = sb("tmpY", [128, 2048], F32, ps_)
            S.op('pool', lambda E: E.memset(st[:], 0.0), w=['st'])
            S.op('pool', lambda E: E.memset(st_bf[:], 0.0), w=['st_bf'])
            order = [1, 0] + list(range(17, 1, -1))
            for i, tt in enumerate(order):
                xB = xBs[i % 2]
                S.dma(q_sp, SBST[tt], st_bf[:], r=['st_bf'], w=['SBST'], semkey='oSB', acc=True)
                if i == len(order) - 1:
                    break
                S.dma(q_sp, xB[:], XBM[tt * 128:(tt + 1) * 128, :], w=['xB'], semkey='xB%d' % (i % 2))
                for _ in ssd_small(tt, wk):
                    pass
                for _ in ssd_state_update(xB, 'xB', xw, 32, 160, st, st_bf, tmpY, wk):
                    pass
            S.barrier()

    def merge_branch(yT, ykey, nk, Wo, wokey, br, gt, mT, tmpm, first, gkey='gt', mkey='mT', banks=(0, 1)):
        for fo in range(8):
            for k in range(nk):
                S.op('pe', lambda E, fo=fo, k=k: E.matmul(PS(banks[fo // 4], (fo % 4) * 128, 128), lhsT=Wo[:, k, fo * 128:(fo + 1) * 128], rhs=yT[:, k, :],
                                                          start=(k == 0), stop=(k == nk - 1)), r=[wokey, ykey], w=PK(banks[fo // 4]))
        for b in range(2):
            gv = gt[:, br * 8 + b * 4:br * 8 + b * 4 + 4, :].rearrange("p a t -> p (a t)")
            mv = mT[:, b * 4:b * 4 + 4, :].rearrange("p a t -> p (a t)")
            if first:
                S.op('dve', lambda E, b=b, gv=gv, mv=mv: E.tensor_tensor(out=mv, in0=PS(banks[b]), in1=gv, op=ALU.mult), r=PK(banks[b]) + [gkey], w=[mkey])
            else:
                tv = tmpm[:, b * 4:b * 4 + 4, :].rearrange("p a t -> p (a t)")
                S.op('dve', lambda E, b=b, gv=gv, tv=tv: E.tensor_tensor(out=tv, in0=PS(banks[b]), in1=gv, op=ALU.mult), r=PK(banks[b]) + [gkey], w=['tmpm'])
                S.op('pool', lambda E, mv=mv, tv=tv: E.tensor_tensor(out=mv, in0=mv, in1=tv, op=ALU.add), r=['tmpm', mkey], w=[mkey])

    def load_Wo(dst, src, key, nk):
        for k0 in range(0, nk, 4):
            S.dma(q_cast, dst[:, k0:k0 + 4, :], src[k0 * 128:(k0 + 4) * 128, :].rearrange("(k p) n -> p k n", p=128), w=[key], acc=True)

    MTS = dscr("MTS", [NT, 128, 8 * 128], F32)
    W1S = dscr("W1S", [8, 128, KC * 512], BF16)

    def phase_ssd_fwd(l):
        last = (l == DEPTH - 1)
        with ExitStack() as ps_:
            Wo = sb("Wo_ssd", [128, 16, D], BF16, ps_); sng = sb("sng", [128, 16], F32, ps_)
            sets = []
            for p in range(2):
                x = '_%d' % p
                sets.append(dict(sfx=x, wk=ssd_wk(ps_, x), xB=sb("xB" + x, [128, 3072], BF16, ps_), bct=sb("bct" + x, [128, 16, 128], BF16, ps_),
                                 zt=sb("zt" + x, [128, 2048], BF16, ps_), sbst=sb("sbst" + x, [128, 2048], BF16, ps_),
                                 gt=sb("gt" + x, [128, 8, 128], BF16, ps_), ahl=sb("ahl" + x, [128, 2, 64], BF16, ps_),
                                 CBm=[sb("CBm%d%s" % (i, x), [128, 8, 128], F32, ps_) for i in range(2)],
                                 xdt=[sb("xdt%d%s" % (i, x), [128, 2048], BF16, ps_) for i in range(2)]))
            st = sb("st", [128, 2048], F32, ps_); st_bf = sb("st_bf", [128, 2048], BF16, ps_)
            xw = sb("xw", [128, 2048], BF16, ps_)
            Lt = [[sb("Lt%d%d" % (d_, i), [128, 512], F32, ps_) for i in range(2)] for d_ in range(2)]
            Mt = [[sb("Mt%d%d" % (d_, i), [128, 512], BF16, ps_) for i in range(2)] for d_ in range(2)]
            Y = sb("Y", [128, 2048], F32, ps_); tmpY = sb("tmpY", [128, 2048], F32, ps_)
            yn = sb("yn", [128, 2048], BF16, ps_); yaT = sb("yaT", [128, 16, 128], BF16, ps_)
            ssq = sb("ssq", [128, 8], F32, ps_); junk = sb("junk2", [128, 256], BF16, ps_)
            mT = sb("mT", [128, 8, 128], F32, ps_); tmpY2 = sb("tmpY2", [128, 1024], F32, ps_)
            S.dma(q_sp, sng[:], sng_in[l], w=['sng'])
            load_Wo(Wo, w_o_ssd[l], 'Wo', 16)

            def scale_Wo():
                for k in range(16):
                    S.op('act', lambda E, k=k: E.activation(out=Wo[:, k, :], in_=Wo[:, k, :], func=AF.Copy, scale=sng[:, k:k + 1]),
                         r=['Wo', 'sng'], w=['Wo'])
            S.op('pool', lambda E: E.memset(st[:], 0.0), w=['st'])
            S.op('pool', lambda E: E.memset(st_bf[:], 0.0), w=['st_bf'])

            def need_y(tt):
                return not (last and tt < 2)

            def stage_A(tt):
                s_ = sets[tt % 2]; x = s_['sfx']; wk = s_['wk']
                xB, bct, zt, sbst, gt, ahl, CBm, xdt = (s_[k] for k in ('xB', 'bct', 'zt', 'sbst', 'gt', 'ahl', 'CBm', 'xdt'))
                S.dma(q_sp, xB[:], XBM[tt * 128:(tt + 1) * 128, :], w=['xB' + x])
                for _ in ssd_small(tt, wk):
                    pass
                if not need_y(tt):
                    return
                S.dma(q_sp, bct[:], BCT[:, tt * 128:(tt + 1) * 128].rearrange("(c p) t -> p c t", p=128), w=['bct' + x])
                S.dma(q_sp, sbst[:], SBST[tt], w=['sbst' + x])
                S.dma(q_sp, zt[:], ZTM[tt * 128:(tt + 1) * 128, :], w=['zt' + x])
                S.dma(q_sp, gt[:], GFM[0:1024, tt * 128:(tt + 1) * 128].rearrange("(c p) t -> p c t", p=128), w=['gt' + x])
                a2 = wk['a2']
                S.op('act', lambda E: E.activation(out=ahl[:, 0, :], in_=a2[:], func=AF.Copy), r=['a2' + x], w=['ahl' + x])
                S.op('dve', lambda E: E.tensor_tensor(out=ahl[:, 1, :], in0=a2[:], in1=ahl[:, 0, :], op=ALU.subtract), r=['a2' + x, 'ahl' + x], w=['ahl' + x])
                for g in range(8):
                    S.op('pe', lambda E, g=g: E.matmul(PS(6 + g // 4, (g % 4) * 128, 128), lhsT=bct[:, g, :], rhs=bct[:, 8 + g, :], start=True, stop=True),
                         r=['bct' + x], w=PK(6 + g // 4))
                for dr, cm in enumerate((C_MF01, C_MB01)):
                    S.op('dve', lambda E, dr=dr, cm=cm: E.tensor_tensor(out=CBm[dr][:], in0=psall[:, 6 * 512:8 * 512].rearrange("p (g t) -> p g t", t=128),
                                                                      in1=cst[:, cm, :].unsqueeze(1).broadcast_to([128, 8, 128]), op=ALU.mult),
                         r=PK(6, 7) + ['cst'], w=['CBm%d%s' % (dr, x)])
                for dr in range(2):
                    S.op('pool', lambda E, dr=dr: E.tensor_tensor(out=v3(xdt[dr][:], 64), in0=v3(xB[:, 0:2048], 64),
                                                                 in1=bc3(dt_all[:, tt, dr * 32:dr * 32 + 32], 64), op=ALU.mult),
                         r=['xB' + x, 'dt_all'], w=['xdt%d%s' % (dr, x)])

            def stage_B(tt):
                if not need_y(tt):
                    return
                s_ = sets[tt % 2]; x = s_['sfx']
                ahl, CBm, xdt = s_['ahl'], s_['CBm'], s_['xdt']

                def emit_E(g):
                    sl = g % 2
                    for dr in range(2):
                        c1 = C_MLE if dr == 0 else C_NMLT
                        c2 = C_NMLE if dr == 0 else C_LW
                        cmk = C_NMF if dr == 0 else C_NMB
                        for hh in range(4):
                            hd = dr * 32 + g * 4 + hh
                            Er = PS(dr, hh * 128, 128)
                            for v in range(2):
                                S.op('pe', lambda E, Er=Er, hd=hd, v=v, c1=c1: E.matmul(Er, lhsT=ahl[:, v, hd:hd + 1].broadcast_to([128, 128]), rhs=cstb[:, c1, :],
                                                                                      start=(v == 0), stop=False), r=['ahl' + x, 'cstb'], w=PK(dr))
                                S.op('pe', lambda E, Er=Er, hd=hd, v=v, c2=c2: E.matmul(Er, lhsT=cstb[:, c2, :], rhs=ahl[:, v, hd:hd + 1].broadcast_to([128, 128]),
                                                                                      start=False, stop=False), r=['ahl' + x, 'cstb'], w=PK(dr))
                            S.op('pe', lambda E, Er=Er, cmk=cmk: E.matmul(Er, lhsT=identb, rhs=cstb[:, cmk, :], start=False, stop=True), r=['cstb'], w=PK(dr))
                        S.op('act', lambda E, dr=dr, sl=sl: E.activation(out=Lt[dr][sl][:], in_=PS(dr), func=AF.Exp), r=PK(dr), w=['Lt%d%d' % (dr, sl)])
                        S.op('dve' if (dr == 0 or g % 2 == 0) else 'pool', lambda E, dr=dr, sl=sl, g=g: E.tensor_tensor(out=Mt[dr][sl][:].rearrange("p (h t) -> p h t", t=128),
                                                                               in0=Lt[dr][sl][:].rearrange("p (h t) -> p h t", t=128),
                                                                               in1=CBm[dr][:, g, :].unsqueeze(1).broadcast_to([128, 4, 128]), op=ALU.mult),
                             r=['Lt%d%d' % (dr, sl), 'CBm%d%s' % (dr, x)], w=['Mt%d%d' % (dr, sl)])

                def emit_Y(g):
                    sl = g % 2
                    for hh in range(4):
                        h = g * 4 + hh
                        for dr in range(2):
                            S.op('pe', lambda E, sl=sl, dr=dr, h=h, hh=hh: E.matmul(PS(2 + h // 8, (h % 8) * 64, 64), lhsT=Mt[dr][sl][:, hh * 128:(hh + 1) * 128],
                                                                                 rhs=xdt[dr][:, h * 64:(h + 1) * 64], start=(dr == 0), stop=(dr == 1)),
                                 r=['Mt%d%d' % (dr, sl), 'xdt%d%s' % (dr, x)], w=PK(2 + h // 8))

                emit_E(0)
                yield
                for g in range(1, 8):
                    emit_E(g)
                    emit_Y(g - 1)
                    yield
                emit_Y(7)
                yield

            def stage_C0(tt):
                if not need_y(tt):
                    return
                s_ = sets[tt % 2]; x = s_['sfx']; wk = s_['wk']
                bct, sbst = s_['bct'], s_['sbst']
                S.op('dve', lambda E: E.tensor_copy(out=Y[:], in_=psall[:, 2 * 512:6 * 512]), r=PK(2, 3, 4, 5), w=['Y'])
                for dr, (stb, skey) in enumerate(((st_bf, 'st_bf'), (sbst, 'sbst' + x))):
                    for half in range(2):
                        yield
                        for gg in range(4):
                            g = half * 4 + gg
                            S.op('pe', lambda E, g=g, gg=gg, stb=stb: E.matmul(PS(6 + gg // 2, (gg % 2) * 256, 256), lhsT=bct[:, 8 + g, :],
                                                                               rhs=stb[:, g * 256:(g + 1) * 256], start=True, stop=True),
                                 r=['bct' + x, skey], w=PK(6 + gg // 2))
                        c0 = half * 1024
                        S.op('dve', lambda E, c0=c0, dr=dr, half=half: E.tensor_tensor(
                            out=v3(tmpY[:, c0:c0 + 1024], 64), in0=v3(psall[:, 6 * 512:8 * 512], 64),
                            in1=bc3(wk['e64'][:, dr * 32 + half * 16:dr * 32 + half * 16 + 16], 64), op=ALU.mult),
                            r=PK(6, 7) + ['e64' + x], w=['tmpY'])
                        S.op('pool', lambda E, c0=c0: E.tensor_tensor(out=Y[:, c0:c0 + 1024], in0=Y[:, c0:c0 + 1024], in1=tmpY[:, c0:c0 + 1024], op=ALU.add),
                             r=['Y', 'tmpY'], w=['Y'])

            def stage_C1(tt):
                s_ = sets[tt % 2]; x = s_['sfx']; wk = s_['wk']
                xB, zt, gt = s_['xB'], s_['zt'], s_['gt']
                if need_y(tt):
                    yield
                    S.op('pool', lambda E: E.tensor_tensor(out=v3(tmpY[:], 64), in0=v3(xB[:, 0:2048], 64), in1=bc3(Dbc[:, 0:32], 64), op=ALU.mult),
                         r=['xB' + x, 'Dbc'], w=['tmpY'])
                    S.op('pool', lambda E: E.tensor_tensor(out=Y[:], in0=Y[:], in1=tmpY[:], op=ALU.add), r=['Y', 'tmpY'], w=['Y'])
                    S.op('pool', lambda E: E.tensor_tensor(out=Y[:], in0=Y[:], in1=zt[:], op=ALU.mult), r=['Y', 'zt' + x], w=['Y'])
                    yield
                    for g in range(8):
                        S.op('act', lambda E, g=g: E.activation(out=junk[:], in_=Y[:, g * 256:(g + 1) * 256], func=AF.Square, accum_out=ssq[:, g:g + 1]),
                             r=['Y'], w=['junk2', 'ssq'])
                    S.op('dve', lambda E: E.tensor_scalar(out=ssq[:], in0=ssq[:], scalar1=1.0 / 256, scalar2=EPS, op0=ALU.mult, op1=ALU.add), r=['ssq'], w=['ssq'])
                    S.op('act', lambda E: E.activation(out=ssq[:], in_=ssq[:], func=AF.Sqrt), r=['ssq'], w=['ssq'])
                    S.op('dve', lambda E: E.reciprocal(out=ssq[:], in_=ssq[:]), r=['ssq'], w=['ssq'])
                    S.op('dve', lambda E: E.tensor_tensor(out=v3(yn[:], 256), in0=v3(Y[:], 256), in1=bc3(ssq[:, 0:8], 256), op=ALU.mult), r=['Y', 'ssq'], w=['yn'])
                    yield
                    for k in range(16):
                        S.op('pe', lambda E, k=k: E.transpose(out=PSB(6 + k // 8)[:, (k % 8) * 128:(k % 8 + 1) * 128], in_=yn[:, k * 128:(k + 1) * 128], identity=identb),
                             r=['yn', 'cstb'], w=PK(6 + k // 8))
                    for b in range(2):
                        S.op('act', lambda E, b=b: E.activation(out=yaT[:, b * 8:(b + 1) * 8, :].rearrange("p a t -> p (a t)"), in_=PSB(6 + b), func=AF.Copy),
                             r=PK(6 + b), w=['yaT'])
                    yield
                    for fo in range(8):
                        if fo == 4:
                            yield
                        for k in range(16):
                            S.op('pe', lambda E, fo=fo, k=k: E.matmul(PS(6 + fo // 4, (fo % 4) * 128, 128), lhsT=Wo[:, k, fo * 128:(fo + 1) * 128], rhs=yaT[:, k, :],
                                                                      start=(k == 0), stop=(k == 15)), r=['Wo', 'yaT'], w=PK(6 + fo // 4))
                    for b in range(2):
                        gv = gt[:, b * 4:b * 4 + 4, :].rearrange("p a t -> p (a t)")
                        mv = mT[:, b * 4:b * 4 + 4, :].rearrange("p a t -> p (a t)")
                        S.op('dve', lambda E, b=b, gv=gv, mv=mv: E.tensor_tensor(out=mv, in0=PS(6 + b), in1=gv, op=ALU.mult), r=PK(6 + b) + ['gt' + x], w=['mT'])
                    pending_st.append(tt)

            pending_st = []

            def flush_stores():
                while pending_st:
                    t_ = pending_st.pop(0)
                    S.dma(q_sp, MTS[t_], mT[:].rearrange("p a t -> p (a t)"), r=['mT'], w=['MTS'], semkey='oMT', acc=True)

            def stage_SU(tt):
                s_ = sets[tt % 2]
                if tt != NT - 1:
                    for _ in ssd_state_update(s_['xB'], 'xB' + s_['sfx'], xw, 0, 128, st, st_bf, tmpY2, s_['wk']):
                        pass

            def stage_C(tt):
                yield from stage_C0(tt)
                yield
                stage_SU(tt)
                yield from stage_C1(tt)

            def interleave(*gens):
                gens = [g for g in gens if g is not None]
                while gens:
                    for g in list(gens):
                        try:
                            next(g)
                        except StopIteration:
                            gens.remove(g)

            stage_A(0)
            if need_y(0):
                interleave(stage_B(0))
                scale_Wo()
            else:
                scale_Wo()
            for tt in range(NT):
                if tt + 1 < NT:
                    stage_A(tt + 1)
                    flush_stores()
                    interleave(stage_C(tt), stage_B(tt + 1))
                else:
                    flush_stores()
                    interleave(stage_C(tt))
            flush_stores()
            S.barrier()

    def phase_attn(l, kind):
        last = (l == DEPTH - 1)
        na = (kind == 'na')
        nkv = 16 if na else 4
        R = 6 if na else 4
        br = 1 if na else 2
        with ExitStack() as ps_:
            Wo = sb("Wo_att", [128, 8, D], BF16, ps_)
            load_Wo(Wo, (w_o_na if na else w_o_swa)[l], 'Wo', 8)
            if na:
                nabt = sb("nabt", [128, 16, NB_NA, 128], BF16, ps_)
                nst = [sb("nst%d" % i, [128, NB_NA, 128], F32, ps_) for i in range(2)]

                def load_nabt():
                    for h in range(16):
                        st_ = nst[h % 2]; nk = 'nst%d' % (h % 2)
                        S.dma(q_sp, st_[:], nab_in[l, h].rearrange("b k q -> k b q"), w=[nk])
                        S.op('act', lambda E, h=h, st_=st_: E.activation(out=nabt[:, h, :, :], in_=st_[:], func=AF.Copy), r=[nk], w=['nabt'])
            else:
                wout = sb("wout", [128, 8, D], BF16, ps_)
                load_Wo(wout, w_out[l], 'wout', 8)
                xts = [sb("xt%d" % i, [128, D], F32, ps_) for i in range(2)]
                tmpx = sb("tmpx", [128, D], F32, ps_); mTb = sb("mTb", [128, 8, 128], BF16, ps_)
            kts = [sb("kt%d" % i, [128, 8 if na else 4, 128], BF16, ps_) for i in range(R + 2)]
            vts = [sb("vt%d" % i, [128, nkv, 64], BF16, ps_) for i in range(R + 2)]
            qTs = [sb("qT%d" % i, [128, 8, 2, 128], BF16, ps_) for i in range(2)]
            gts = [sb("gt%d" % i, [128, 8, 128], BF16, ps_) for i in range(2)]
            mTs = [sb("mT%d" % i, [128, 8, 128], F32, ps_) for i in range(2)]
            for i in range(2):
                S.op('pool', lambda E, i=i: E.memset(qTs[i][:], 0.0), w=['qTz%d' % i])
            nkmax = (5 if na else 3) + 2
            Eb = [sb("Eb%d" % i, [128, nkmax, 1024], BF16, ps_) for i in range(2)]
            den = sb("den", [128, 16], F32, ps_); yatt = sb("yatt", [128, D], BF16, ps_); yT = sb("yT", [128, 8, 128], BF16, ps_)
            tmpm = sb("tmpm", [128, 8, 128], F32, ps_)
            KS = NAK if na else SWK; VS = NAV if na else SWV; QS = NAQ if na else SWQ

            def load_kv(i, ktt):
                kk = 'kt%d' % i; vk = 'vt%d' % i
                if na:
                    S.dma(q_sp, kts[i][:], KS[:, ktt * 128:(ktt + 1) * 128].rearrange("(k p) t -> p k t", p=128), w=[kk])
                else:
                    for hf in range(2):
                        S.dma(q_sp, kts[i][hf * 64:(hf + 1) * 64, :, :], KS[:, ktt * 128:(ktt + 1) * 128].rearrange("(g d) t -> d g t", d=64), w=[kk], acc=(hf == 1))
                S.dma(q_sp, vts[i][:].rearrange("p h d -> p (h d)"), VS[ktt * 128:(ktt + 1) * 128, :], w=[vk])

            def local_tiles(t):
                if na:
                    return na_key_tiles(t)
                return [u for u in (t - 1, t, t + 1) if 0 <= u <= 15]

            load_kv(R, 0); load_kv(R + 1, 1)
            ids = na_ids()
            tiles = [tt for tt in range(NT) if not (last and tt < 2)]
            ring = {'upto': -1}

            def issue_loads(idx):
                tt = tiles[idx]; p = idx % 2
                qsrc = QS[:, tt * 128:(tt + 1) * 128].rearrange("(k p) t -> p k t", p=128)
                S.dma(q_sp, qTs[p][0:64, :, 0, :], qsrc[0:64], r=['qTz%d' % p], w=['qTa%d' % p])
                S.dma(q_sp, qTs[p][64:128, :, 1, :], qsrc[64:128], r=['qTz%d' % p], w=['qTb%d' % p])
                S.dma(q_sp, gts[p][:], GFM[br * 1024:(br + 1) * 1024, tt * 128:(tt + 1) * 128].rearrange("(c p) t -> p c t", p=128), w=['gt%d' % p])
                S.dma(q_sp, mTs[p][:].rearrange("p a t -> p (a t)"), MTS[tt], r=['MTS'], w=['mT%d' % p])
                if not na:
                    S.dma(q_sp, xts[p][:], xres[tt * 128:(tt + 1) * 128, :], r=['xres'], w=['xt%d' % p])
                if tt >= 2:
                    for u in local_tiles(tt - 2):
                        if u > ring['upto']:
                            load_kv(u % R, u + 2)
                            ring['upto'] = u

            issue_loads(0)
            if na:
                load_nabt()
            scst = {'sc': 0}
            for idx, tt in enumerate(tiles):
                if idx + 1 < len(tiles):
                    issue_loads(idx + 1)
                p = idx % 2
                qT = qTs[p]; gt = gts[p]; mT = mTs[p]
                qk = ['qTa%d' % p, 'qTb%d' % p, 'qTz%d' % p]

                def make_keys(tt):
                    keys = []
                    if tt >= 2:
                        t = tt - 2
                        if na:
                            for u in local_tiles(t):
                                bid = ids[(t, u)]
                                keys.append((u % R, (lambda h, bid=bid: nabt[:, h, bid, :]), 'nabt'))
                        else:
                            for u in local_tiles(t):
                                cm = C_SWLO if u == t - 1 else (C_SWHI if u == t + 1 else None)
                                keys.append((u % R, (None if cm is None else (lambda h, cm=cm: cstb[:, cm, :])), 'cstb'))
                    keys.append((R, None, None)); keys.append((R + 1, None, None))
                    return keys

                keys = make_keys(tt)
                nk_ = len(keys)
                def emit_scores(hb, keys, qT, qk, scst):
                    eb = Eb[hb % 2]; ek = 'Eb%d' % (hb % 2)
                    for ui, (bi, bfn, bkey) in enumerate(keys):
                        base = 0 if scst['sc'] % 2 == 0 else 6
                        scst['sc'] += 1
                        for i in range(8):
                            h = hb * 8 + i
                            bank = base + (i % 2); off = (i // 2) * 128
                            kt = kts[bi][:, h // 2, :] if na else kts[bi][:, h // 4, :]
                            S.op('pe', lambda E, bank=bank, off=off, kt=kt, h=h, bfn=bfn: E.matmul(PS(bank, off, 128), lhsT=kt, rhs=qT[:, h // 2, h % 2, :],
                                                                                               start=True, stop=(bfn is None)),
                                 r=['kt%d' % bi] + qk, w=PK(bank))
                            if bfn is not None:
                                S.op('pe', lambda E, bank=bank, off=off, h=h, bfn=bfn: E.matmul(PS(bank, off, 128), lhsT=identb, rhs=bfn(h), start=False, stop=True),
                                     r=['cstb', bkey], w=PK(bank))
                        S.op('act', lambda E, base=base, ui=ui, eb=eb: E.activation(out=eb[:, ui, :], in_=psall[:, base * 512:base * 512 + 1024], func=AF.Exp),
                             r=PK(base, base + 1), w=[ek])

                if idx == 0:
                    emit_scores(0, keys, qT, qk, scst)
                emit_scores(1, keys, qT, qk, scst)
                for hb in range(2):
                    eb = Eb[hb % 2]; ek = 'Eb%d' % (hb % 2)
                    for i in range(8):
                        h = hb * 8 + i
                        eoff = (i % 2) * 512 + (i // 2) * 128
                        hv = h if na else h // 4
                        ob = 2 + h // 8
                        for ui, (bi, bfn, bkey) in enumerate(keys):
                            S.op('pe', lambda E, ob=ob, h=h, hv=hv, ui=ui, bi=bi, eoff=eoff, eb=eb: E.matmul(PS(ob, (h % 8) * 64, 64), lhsT=eb[:, ui, eoff:eoff + 128],
                                                                                                         rhs=vts[bi][:, hv, :], start=(ui == 0), stop=(ui == nk_ - 1)),
                                 r=[ek, 'vt%d' % bi], w=PK(ob))
                            S.op('pe', lambda E, h=h, ui=ui, eoff=eoff, eb=eb: E.matmul(PS(4, h * 2, 2), lhsT=eb[:, ui, eoff:eoff + 128],
                                                                                       rhs=cstb[:, C_ONES, 0:2], start=(ui == 0), stop=(ui == nk_ - 1)),
                                 r=[ek, 'cstb'], w=PK(4))
                S.op('dve', lambda E: E.tensor_copy(out=den[:], in_=PS(4, 0, 32).rearrange("p (h two) -> p h two", two=2)[:, :, 0]), r=PK(4), w=['den'])
                if not na:
                    S.op('dve', lambda E: E.tensor_tensor(out=den[:], in0=den[:], in1=sinkexp[:], op=ALU.add), r=['den', 'sinkexp'], w=['den'])
                S.op('dve', lambda E: E.reciprocal(out=den[:], in_=den[:]), r=['den'], w=['den'])
                for b in range(2):
                    S.op('dve', lambda E, b=b: E.tensor_tensor(out=v3(yatt[:, b * 512:(b + 1) * 512], 64), in0=v3(PS(2 + b), 64),
                                                               in1=bc3(den[:, b * 8:b * 8 + 8], 64), op=ALU.mult), r=PK(2 + b) + ['den'], w=['yatt'])
                if idx + 1 < len(tiles):
                    pn = (idx + 1) % 2
                    emit_scores(0, make_keys(tiles[idx + 1]), qTs[pn], ['qTa%d' % pn, 'qTb%d' % pn, 'qTz%d' % pn], scst)
                for k in range(8):
                    S.op('pe', lambda E, k=k: E.transpose(out=PSB(5)[:, k * 128:(k + 1) * 128], in_=yatt[:, k * 128:(k + 1) * 128], identity=identb),
                         r=['yatt', 'cstb'], w=PK(5))
                S.op('act', lambda E: E.activation(out=yT[:].rearrange("p a t -> p (a t)"), in_=PSB(5), func=AF.Copy), r=PK(5), w=['yT'])
                merge_branch(yT, 'yT', 8, Wo, 'Wo', 0, gt, mT, tmpm, False, gkey='gt%d' % p, mkey='mT%d' % p, banks=(4, 5))
                if na:
                    S.dma(q_sp, MTS[tt], mT[:].rearrange("p a t -> p (a t)"), r=['mT%d' % p], w=['MTS'], semkey='oMT%d' % p, acc=True)
                else:
                    tk = tok_type(tt)
                    xt = xts[p]; xk = 'xt%d' % p
                    S.op('act', lambda E, mT=mT: E.activation(out=mTb[:], in_=mT[:], func=AF.Copy), r=['mT%d' % p], w=['mTb'])
                    for half in range(2):
                        for k in range(8):
                            S.op('pe', lambda E, half=half, k=k: E.matmul(PS(4 + half), lhsT=mTb[:, k, :], rhs=wout[:, k, half * 512:(half + 1) * 512],
                                                                        start=(k == 0), stop=(k == 7)), r=['mTb', 'wout'], w=PK(4 + half))
                        hs = slice(half * 512, (half + 1) * 512)
                        S.op('dve', lambda E, half=half, hs=hs, tk=tk: E.tensor_tensor(out=tmpx[:, hs], in0=PS(4 + half), in1=gbc[:, tk, 0, hs], op=ALU.mult),
                             r=PK(4 + half) + ['gbc'], w=['tmpx'])
                        S.op('pool', lambda E, hs=hs, xt=xt: E.tensor_tensor(out=xt[:, hs], in0=xt[:, hs], in1=tmpx[:, hs], op=ALU.add), r=['tmpx', xk], w=[xk])
                    S.dma(q_sp, xres[tt * 128:(tt + 1) * 128, :], xt[:], r=[xk], w=['xres'], semkey='oxres%d' % p, acc=True)
            S.barrier()

    def phase_ffn(l):
        last = (l == DEPTH - 1)
        with ExitStack() as ps_:
            w2 = sb("w2", [128, 32, D], BF16, ps_)
            w1 = [sb("w1_%d" % i, [128, KC, 512], BF16, ps_) for i in range(2)]
            h2Ts = [sb("h2T%d" % i, [128, KC, 512], BF16, ps_) for i in range(2)]
            uT = sb("uT", [128, 32, 512], BF16, ps_)
            xns = [sb("xfn%d" % i, [128, D], F32, ps_) for i in range(2)]
            xus = [sb("xfu%d" % i, [128, D], F32, ps_) for i in range(2)]
            rl = [sb("rl%d" % i, [128, 512], F32, ps_) for i in range(2)]
            tmpx = sb("tmpx", [128, D], F32, ps_)
            wk = dict(junk=sb("junk", [128, D], BF16, ps_), ss=sb("ss", [128, 1], F32, ps_), xn=sb("xn", [128, D], BF16, ps_))
            blocks = [b_ for b_ in (1, 2, 3, 4, 0) if not (last and b_ == 0)]
            st_ = {'wi': 0, 'ri': 0, 'ni': 0, 'ui': 0, 'w2': False}

            def do_norm(bi):
                blk = blocks[bi]; t0, n = BLKS[blk]
                h2T = h2Ts[bi % 2]; hk = 'h2T%d' % (bi % 2)
                for i in range(n // 128):
                    tt = t0 // 128 + i
                    j = st_['ni'] % 2; st_['ni'] += 1
                    S.dma(q_sp, xns[j][:], xres[tt * 128:(tt + 1) * 128, :], r=['xres'], w=['xfn%d' % j])
                    norm_to_T(l, tt, xns[j][:], 'xfn%d' % j, G2, S2, h2T, hk, i * 128, wk)

            def do_ffn1(bi):
                blk = blocks[bi]; t0, n = BLKS[blk]
                h2T = h2Ts[bi % 2]; hk = 'h2T%d' % (bi % 2)
                for g in range(8):
                    w = w1[st_['wi'] % 2]; wkey = 'w1_%d' % (st_['wi'] % 2); st_['wi'] += 1
                    if bi == 0:
                        cast_load(w[:], w_ff1[l, :, g * 512:(g + 1) * 512].rearrange("(k p) n -> p k n", p=128), wkey)
                        S.dma(q_sp, W1S[g], w[:].rearrange("p k n -> p (k n)"), r=[wkey], w=['W1S'], semkey='o' + wkey, acc=True)
                    else:
                        S.dma(q_sp, w[:].rearrange("p k n -> p (k n)"), W1S[g], r=['W1S'], w=[wkey])
                    if g == 1 and not st_['w2']:
                        st_['w2'] = True
                        load_Wo(w2, w_ff2[l], 'w2', 32)
                    for j in range(4):
                        c = g * 4 + j
                        bank = c % 4
                        for k in range(KC):
                            S.op('pe', lambda E, k=k, j=j, w=w, bank=bank, n=n: E.matmul(PS(bank, 0, n), lhsT=w[:, k, j * 128:(j + 1) * 128], rhs=h2T[:, k, 0:n],
                                                                                       start=(k == 0), stop=(k == KC - 1)), r=[wkey, hk], w=PK(bank))
                        r_ = rl[st_['ri'] % 2]; rk = 'rl%d' % (st_['ri'] % 2); st_['ri'] += 1
                        S.op('act', lambda E, r_=r_, bank=bank, n=n: E.activation(out=r_[:, 0:n], in_=PS(bank, 0, n), func=AF.Relu), r=PK(bank), w=[rk])
                        S.op('pool', lambda E, r_=r_, c=c, n=n: E.tensor_tensor(out=uT[:, c, 0:n], in0=r_[:, 0:n], in1=r_[:, 0:n], op=ALU.mult), r=[rk], w=['uT'])

            def do_ffn2(bi):
                blk = blocks[bi]; t0, n = BLKS[blk]
                for i in range(n // 128):
                    tt = t0 // 128 + i
                    tk = tok_type(tt)
                    j = st_['ui'] % 2; st_['ui'] += 1
                    xu = xus[j]; xk = 'xfu%d' % j
                    S.dma(q_sp, xu[:], xres[tt * 128:(tt + 1) * 128, :], r=['xres'], w=[xk])
                    for half in range(2):
                        for c in range(32):
                            S.op('pe', lambda E, half=half, c=c, i=i: E.matmul(PS(6 + half), lhsT=uT[:, c, i * 128:(i + 1) * 128], rhs=w2[:, c, half * 512:(half + 1) * 512],
                                                                             start=(c == 0), stop=(c == 31)), r=['uT', 'w2'], w=PK(6 + half))
                        hs = slice(half * 512, (half + 1) * 512)
                        S.op('dve', lambda E, half=half, hs=hs, tk=tk: E.tensor_tensor(out=tmpx[:, hs], in0=PS(6 + half), in1=gbc[:, tk, 1, hs], op=ALU.mult),
                             r=PK(6 + half) + ['gbc'], w=['tmpx'])
                        S.op('pool', lambda E, hs=hs, xu=xu: E.tensor_tensor(out=xu[:, hs], in0=xu[:, hs], in1=tmpx[:, hs], op=ALU.add), r=['tmpx', xk], w=[xk])
                    S.dma(q_sp, xres[tt * 128:(tt + 1) * 128, :], xu[:], r=[xk], w=['xres'], semkey='o' + xk, acc=True)

            do_norm(0)
            for bi in range(len(blocks)):
                do_ffn1(bi)
                if bi + 1 < len(blocks):
                    do_norm(bi + 1)
                do_ffn2(bi)
            S.barrier()

    def phase_final():
        with ExitStack() as ps_:
            xts = [sb("xo%d" % i, [128, D], F32, ps_) for i in range(2)]
            junk = sb("junk", [128, D], BF16, ps_); ss = sb("ss", [128, 1], F32, ps_)
            for t in range(16):
                xt = xts[t % 2]; xk = 'xo%d' % (t % 2)
                S.dma(q_sp, xt[:], xres[(t + 2) * 128:(t + 3) * 128, :], r=['xres'], w=[xk])
                S.op('act', lambda E, xt=xt: E.activation(out=junk[:], in_=xt[:], func=AF.Square, accum_out=ss[:]), r=[xk], w=['junk', 'ss'])
                S.op('dve', lambda E: E.tensor_scalar(out=ss[:], in0=ss[:], scalar1=1.0 / D, scalar2=EPS, op0=ALU.mult, op1=ALU.add), r=['ss'], w=['ss'])
                S.op('act', lambda E: E.activation(out=ss[:], in_=ss[:], func=AF.Sqrt), r=['ss'], w=['ss'])
                S.op('dve', lambda E: E.reciprocal(out=ss[:], in_=ss[:]), r=['ss'], w=['ss'])
                S.op('dve', lambda E, xt=xt: E.scalar_tensor_tensor(out=xt[:], in0=xt[:], scalar=ss[:, 0:1], in1=fing[:], op0=ALU.mult, op1=ALU.mult),
                     r=[xk, 'ss', 'fing'], w=[xk])
                S.dma(q_sp, out_d[t * 128:(t + 1) * 128, :], xt[:], r=[xk], w=['out'], semkey='o' + xk, acc=True)
            S.barrier()

    for l in range(n_layers):
        phase_adaln(l)
        if stop_after == 'adaln':
            break
        phase_proj(l)
        if stop_after == 'proj':
            break
        phase_ssd_fwd(l)
        if stop_after == 'ssdf':
            break
        phase_attn(l, 'na')
        if stop_after == 'na':
            break
        phase_attn(l, 'swa')
        if stop_after == 'swa':
            break
        phase_ffn(l)
    if stop_after is None and n_layers == DEPTH:
        phase_final()

    if dbg:
        dmod = nc.dram_tensor("dbg_modT", [128, 96], F32, kind="ExternalOutput").ap()
        S.dma(q_sp, dmod[:, :], modT[:].rearrange("p j t -> p (j t)"), r=['modT'], w=['dbg_modT'])
        dgbc = nc.dram_tensor("dbg_gbc", [128, 4 * D], F32, kind="ExternalOutput").ap()
        S.dma(q_sp, dgbc[:, :], gbc[:].rearrange("p a b d -> p (a b d)"), r=['gbc'], w=['dbg_gbc'])
        ddt = nc.dram_tensor("dbg_dt", [128, NT * 64], F32, kind="ExternalOutput").ap()
        S.dma(q_sp, ddt[:, :], dt_all[:].rearrange("p a b -> p (a b)"), r=['dt_all'], w=['dbg_dt'])
    S.finish()
    es.close()
    return nc


def prep_inputs(inputs):
    f = lambda a: np.ascontiguousarray(np.asarray(a, dtype=np.float32))
    x = f(inputs['x']); c = f(inputs['c']); ctx = f(inputs['ctx']); c_ctx = f(inputs['c_ctx'])
    shared = {}
    shared['ada_w'] = f(inputs['ada_w'])
    shared['ada_b2'] = f(np.stack([inputs['ada_b'], inputs['ada_b']], 1))
    shared['n1g'] = f(np.asarray(inputs['norm1_g']).reshape(DEPTH, KC, 128).transpose(0, 2, 1))
    shared['n2g'] = f(np.asarray(inputs['norm2_g']).reshape(DEPTH, KC, 128).transpose(0, 2, 1))
    shared['final_g'] = f(np.asarray(inputs['final_g']).reshape(1, D))
    shared['w_in'] = f(inputs['w_in'])
    shared['cw'] = f(np.asarray(inputs['conv_w']).reshape(DEPTH, 5, 32, 128).transpose(0, 3, 2, 1))
    shared['cb'] = f(np.asarray(inputs['conv_b']).reshape(DEPTH, 32, 128).transpose(0, 2, 1))
    shared['dtb'] = f(np.asarray(inputs['dt_bias']).reshape(DEPTH, 1, 64))
    shared['alog'] = f(np.asarray(inputs['a_log']).reshape(DEPTH, 1, 64))
    shared['ssdd'] = f(np.asarray(inputs['ssd_d']).reshape(DEPTH, 1, 32))
    shared['sng'] = f(np.asarray(inputs['ssd_norm_g']).reshape(DEPTH, 16, 128).transpose(0, 2, 1))
    rpb = np.asarray(inputs['na_rpb'], np.float32)
    shared['nab'] = f(np.stack([build_na_tables(rpb[l])[0] for l in range(DEPTH)], 0))
    shared['sink'] = f(np.asarray(inputs['swa_sink']).reshape(DEPTH, 1, 16))
    for k in ('w_o_ssd', 'w_o_na', 'w_o_swa', 'w_out', 'w_ff1', 'w_ff2'):
        shared[k] = f(inputs[k])
    shared['cst'] = host_consts()
    rc, rs = host_rope()
    shared['ropec'] = rc; shared['ropes'] = rs
    maps = []
    for b in range(x.shape[0]):
        m = dict(shared)
        m['x'] = x[b]; m['ctx'] = ctx[b]
        cc = np.stack([c[b].reshape(KC, 128).T, c_ctx.reshape(KC, 128).T], -1)
        m['cc'] = f(cc)
        maps.append(m)
    return maps


def kernel(**inputs):
    maps = prep_inputs(inputs)
    nc = bass.Bass("TRN2", target_bir_lowering=False)
    build(nc)
    res = run_bass_kernel_spmd(nc, maps, core_ids=list(range(8)))
    return np.stack([r['out'] for r in res.results], 0).astype(np.float32)
```
